# Optimizing a Trainium2 kernel written in Bass

```python
import jax, jax.numpy as jnp
from jax import lax
import numpy as np

D_MODEL = 1024
BATCH = 16
SEQ = 4096
DEPTH = 1

CHUNK = 64
GMLP_BLOCK = 128
GMLP_GROUPS = 4
GMLP_GROUP_WIDTH = 128
GMLP_WIDTH = GMLP_GROUPS * GMLP_GROUP_WIDTH
MLA_HEADS = 8
QK_NOPE_DIM = 64
QK_ROPE_DIM = 32
QK_HEAD_DIM = QK_NOPE_DIM + QK_ROPE_DIM
V_HEAD_DIM = 64
Q_LORA_RANK = 256
KV_LORA_RANK = 128
MLA_WIDTH = MLA_HEADS * V_HEAD_DIM
MIX_WIDTH = GMLP_WIDTH + MLA_WIDTH
ROPE_THETA = 10000.0
Q_BLOCK = 128
IN_PROJ_OFFSETS = (512, 1024, 1280, 1408)
IN_PROJ_WIDTH = 1440
FFN_HIDDEN = 2816
RMS_EPS = 1e-6

kernel_name = "hybrid_gmlp_mla_macaron_block"


def rmsnorm(x, g):
    xf = x.astype(jnp.float32)
    y = xf * lax.rsqrt(jnp.mean(xf * xf, axis=-1, keepdims=True) + RMS_EPS)
    return (y * g.astype(jnp.float32)).astype(x.dtype)


def swiglu(h, w_gate, w_up, w_down):
    return (jax.nn.silu(h @ w_gate) * (h @ w_up)) @ w_down


def rope_tables(positions):
    inv_freq = ROPE_THETA ** (-jnp.arange(0, QK_ROPE_DIM, 2, dtype=jnp.float32) / QK_ROPE_DIM)
    ang = positions.astype(jnp.float32)[..., None] * inv_freq
    return jnp.cos(ang)[:, :, None, :], jnp.sin(ang)[:, :, None, :]


def apply_rope(x, cos, sin):
    xf = x.astype(jnp.float32)
    half = QK_ROPE_DIM // 2
    x1, x2 = xf[..., :half], xf[..., half:]
    return jnp.concatenate([x1 * cos - x2 * sin, x2 * cos + x1 * sin], axis=-1).astype(x.dtype)


def gmlp_mixer(u, v, v_norm, w_s, b_s):
    B, S, _ = u.shape
    nb = S // GMLP_BLOCK
    u = jax.nn.gelu(u)
    v = rmsnorm(jax.nn.gelu(v).reshape(B, S, GMLP_GROUPS, GMLP_GROUP_WIDTH), v_norm)
    v = v.reshape(B, nb, GMLP_BLOCK, GMLP_GROUPS, GMLP_GROUP_WIDTH)
    pos = jnp.arange(GMLP_BLOCK)
    mask = (pos[None, :] // CHUNK) <= (pos[:, None] // CHUNK)
    w = jnp.where(mask[None], w_s, jnp.zeros_like(w_s))
    mixed = jnp.einsum('gij,bnjgc->bnigc', w, v) + b_s.T[None, None, :, :, None]
    return u * mixed.reshape(B, S, GMLP_WIDTH)


def block_causal_attention(q, k, v):
    B, S, H, Dq = q.shape
    nqb = S // Q_BLOCK
    q_blocks = (q * (Dq ** -0.5)).reshape(B, nqb, Q_BLOCK, H, Dq).transpose(1, 0, 2, 3, 4)
    k_chunk = jnp.arange(S) // CHUNK

    def one_block(args):
        qb, idx = args
        q_chunk = (idx * Q_BLOCK + jnp.arange(Q_BLOCK)) // CHUNK
        mask = k_chunk[None, :] <= q_chunk[:, None]
        s = jnp.einsum('bqhd,bkhd->bhqk', qb, k).astype(jnp.float32)
        s = jnp.where(mask[None, None], s, -1e30)
        p = jax.nn.softmax(s, axis=-1).astype(v.dtype)
        return jnp.einsum('bhqk,bkhd->bqhd', p, v)

    out = lax.map(one_block, (q_blocks, jnp.arange(nqb)))
    return out.transpose(1, 0, 2, 3, 4).reshape(B, S, H, v.shape[-1])


def mla_mixer(c_q, c_kv, k_rope, cos, sin, q_latent_norm, w_uq, kv_latent_norm, w_ukv,
              q_head_norm, k_head_norm):
    B, S, _ = c_q.shape
    q = (rmsnorm(c_q, q_latent_norm) @ w_uq).reshape(B, S, MLA_HEADS, QK_HEAD_DIM)
    kv = (rmsnorm(c_kv, kv_latent_norm) @ w_ukv).reshape(B, S, MLA_HEADS, QK_NOPE_DIM + V_HEAD_DIM)
    k_nope, v = kv[..., :QK_NOPE_DIM], kv[..., QK_NOPE_DIM:]
    k_r = jnp.broadcast_to(k_rope[:, :, None, :], (B, S, MLA_HEADS, QK_ROPE_DIM))
    k = jnp.concatenate([k_nope, k_r], axis=-1)
    q = rmsnorm(q, q_head_norm)
    k = rmsnorm(k, k_head_norm)
    q = jnp.concatenate([q[..., :QK_NOPE_DIM], apply_rope(q[..., QK_NOPE_DIM:], cos, sin)], axis=-1)
    k = jnp.concatenate([k[..., :QK_NOPE_DIM], apply_rope(k[..., QK_NOPE_DIM:], cos, sin)], axis=-1)
    out = block_causal_attention(q, k, v)
    return out.reshape(B, S, MLA_WIDTH)


def setup_inputs(seed: int = 0) -> dict:
    key = jax.random.key(seed)
    ks = jax.random.split(key, 32)
    f32 = jnp.float32

    def w(k, shape, fan_in):
        return jax.random.normal(k, shape, f32) * (fan_in ** -0.5)

    def gain(k, shape):
        return 1.0 + 0.01 * jax.random.normal(k, shape, f32)

    L = DEPTH
    x = jax.random.normal(ks[0], (BATCH, SEQ, D_MODEL), f32)
    offsets = jax.random.randint(ks[1], (BATCH, 1), 0, 64, dtype=jnp.int32) * CHUNK
    positions = (offsets + jnp.arange(SEQ, dtype=jnp.int32)[None, :]).astype(jnp.int32)
    return {
        "x": x,
        "positions": positions,
        "ffn1_norm": gain(ks[2], (L, D_MODEL)),
        "ffn1_w_gate": w(ks[3], (L, D_MODEL, FFN_HIDDEN), D_MODEL),
        "ffn1_w_up": w(ks[4], (L, D_MODEL, FFN_HIDDEN), D_MODEL),
        "ffn1_w_down": w(ks[5], (L, FFN_HIDDEN, D_MODEL), FFN_HIDDEN),
        "mix_norm": gain(ks[6], (L, D_MODEL)),
        "w_in": w(ks[7], (L, D_MODEL, IN_PROJ_WIDTH), D_MODEL),
        "gmlp_v_norm": gain(ks[8], (L, GMLP_GROUPS, GMLP_GROUP_WIDTH)),
        "gmlp_w_s": w(ks[9], (L, GMLP_GROUPS, GMLP_BLOCK, GMLP_BLOCK), GMLP_BLOCK),
        "gmlp_b_s": gain(ks[10], (L, GMLP_GROUPS, GMLP_BLOCK)),
        "q_latent_norm": gain(ks[11], (L, Q_LORA_RANK)),
        "w_uq": w(ks[12], (L, Q_LORA_RANK, MLA_HEADS * QK_HEAD_DIM), Q_LORA_RANK),
        "kv_latent_norm": gain(ks[13], (L, KV_LORA_RANK)),
        "w_ukv": w(ks[14], (L, KV_LORA_RANK, MLA_HEADS * (QK_NOPE_DIM + V_HEAD_DIM)), KV_LORA_RANK),
        "q_head_norm": gain(ks[15], (L, QK_HEAD_DIM)),
        "k_head_norm": gain(ks[16], (L, QK_HEAD_DIM)),
        "gmlp_out_norm": gain(ks[17], (L, GMLP_WIDTH)),
        "mla_out_norm": gain(ks[18], (L, MLA_WIDTH)),
        "w_out": w(ks[19], (L, MIX_WIDTH, D_MODEL), MIX_WIDTH),
        "ffn2_norm": gain(ks[20], (L, D_MODEL)),
        "ffn2_w_gate": w(ks[21], (L, D_MODEL, FFN_HIDDEN), D_MODEL),
        "ffn2_w_up": w(ks[22], (L, D_MODEL, FFN_HIDDEN), D_MODEL),
        "ffn2_w_down": w(ks[23], (L, FFN_HIDDEN, D_MODEL), FFN_HIDDEN),
        "final_norm": gain(ks[24], (L, D_MODEL)),
    }


def reference(x, positions, ffn1_norm, ffn1_w_gate, ffn1_w_up, ffn1_w_down, mix_norm, w_in,
              gmlp_v_norm, gmlp_w_s, gmlp_b_s, q_latent_norm, w_uq, kv_latent_norm, w_ukv,
              q_head_norm, k_head_norm, gmlp_out_norm, mla_out_norm, w_out,
              ffn2_norm, ffn2_w_gate, ffn2_w_up, ffn2_w_down, final_norm):
    cos, sin = rope_tables(positions)
    o1, o2, o3, o4 = IN_PROJ_OFFSETS
    for l in range(DEPTH):
        x = x + 0.5 * swiglu(rmsnorm(x, ffn1_norm[l]), ffn1_w_gate[l], ffn1_w_up[l], ffn1_w_down[l])
        h = rmsnorm(x, mix_norm[l]) @ w_in[l]
        u, v = h[..., :o1], h[..., o1:o2]
        c_q, c_kv, k_rope = h[..., o2:o3], h[..., o3:o4], h[..., o4:]
        y_a = gmlp_mixer(u, v, gmlp_v_norm[l], gmlp_w_s[l], gmlp_b_s[l])
        y_b = mla_mixer(c_q, c_kv, k_rope, cos, sin, q_latent_norm[l], w_uq[l],
                        kv_latent_norm[l], w_ukv[l], q_head_norm[l], k_head_norm[l])
        y = jnp.concatenate([rmsnorm(y_a, gmlp_out_norm[l]), rmsnorm(y_b, mla_out_norm[l])], axis=-1)
        x = x + y @ w_out[l]
        x = x + 0.5 * swiglu(rmsnorm(x, ffn2_norm[l]), ffn2_w_gate[l], ffn2_w_up[l], ffn2_w_down[l])
        x = rmsnorm(x, final_norm[l])
    return x
```

```python
import math
import numpy as np
import concourse.bass as bass
import concourse.mybir as mybir
from concourse.bass_utils import run_bass_kernel_spmd

F32 = mybir.dt.float32
BF16 = mybir.dt.bfloat16
I32 = mybir.dt.int32
AF = mybir.ActivationFunctionType
ALU = mybir.AluOpType

ENGS = ("pe", "act", "dve", "pool", "sp")

D = 1024
FF = 2816
NFC = 22
T = 512
EPS = 1e-6
NBLK = 46
BLK = 4096
NS = 5
NH = 8
DUMN = 256
DUMW = 512


class Prog:
    def __init__(self, nc):
        self.nc = nc
        self.ops = {e: [] for e in ENGS}
        self.last_w = {}
        self.readers = {}
        self.dma_cnt = {}

    def op(self, eng, fn, r=(), w=()):
        return self._add(eng, fn, r, w, None)

    def dma(self, eng, fn, r=(), w=(), key=None):
        self.dma_cnt[key] = self.dma_cnt.get(key, 0) + 16
        return self._add(eng, fn, r, w, (key, self.dma_cnt[key]))

    def _add(self, eng, fn, r, w, dma):
        deps = set()
        for k in r:
            if k in self.last_w:
                deps.add(self.last_w[k])
        for k in w:
            if k in self.last_w:
                deps.add(self.last_w[k])
            for x in self.readers.get(k, ()):
                deps.add(x)
        ref = (eng, len(self.ops[eng]))
        deps.discard(ref)
        self.ops[eng].append(dict(fn=fn, deps=deps, dma=dma))
        for k in r:
            self.readers.setdefault(k, []).append(ref)
        for k in w:
            self.last_w[k] = ref
            self.readers[k] = []
        return ref

    def emit(self, final_waits=()):
        nc = self.nc
        needed = {e: set() for e in ENGS}
        for e in ENGS:
            for o in self.ops[e]:
                for (de, di) in o["deps"]:
                    if self.ops[de][di]["dma"] is None:
                        if de == "pe" and e == "pe":
                            continue
                        needed[de].add(di)
        rank = {}
        for e in ENGS:
            c = 0
            for i in range(len(self.ops[e])):
                if i in needed[e]:
                    c += 1
                    rank[(e, i)] = c
        sems = {e: nc.alloc_semaphore("s_" + e) for e in ENGS}
        dsems = {k: nc.alloc_semaphore("d_%d" % i) for i, k in enumerate(self.dma_cnt)}

        def target(ref):
            o = self.ops[ref[0]][ref[1]]
            if o["dma"] is not None:
                return dsems[o["dma"][0]], o["dma"][1]
            return sems[ref[0]], rank[ref]

        def run(eng_name, handle):
            seen = {}
            for i, o in enumerate(self.ops[eng_name]):
                for d in sorted(o["deps"]):
                    if d[0] == "pe" and eng_name == "pe" and self.ops[d[0]][d[1]]["dma"] is None:
                        continue
                    s, v = target(d)
                    if seen.get(id(s), 0) < v:
                        handle.wait_ge(s, v)
                        seen[id(s)] = v
                inst = o["fn"](handle)
                if o["dma"] is not None:
                    inst.then_inc(dsems[o["dma"][0]], 16)
                elif (eng_name, i) in rank:
                    inst.then_inc(sems[eng_name], 1)
            if eng_name == "sp":
                for ref in final_waits:
                    s, v = target(ref)
                    handle.wait_ge(s, v)

        with nc.Block() as block:
            @block.tensor
            def _(h):
                run("pe", h)

            @block.scalar
            def _(h):
                run("act", h)

            @block.vector
            def _(h):
                run("dve", h)

            @block.gpsimd
            def _(h):
                run("pool", h)

            @block.sync
            def _(h):
                run("sp", h)


GC_FFN1, GC_MIX, GC_FFN2, GC_FIN = 0, 8, 16, 24
GC_VN, GC_QL, GC_KVL = 32, 36, 38
GC_QH, GC_QHS, GC_KH, GC_KHS = 39, 40, 41, 42
GC_GO, GC_MO = 43, 47
NGC = 55


def build_nc(NSEQ, S):
    NTOK = NSEQ * S
    NTS = S // T
    NKT = S // 128
    nc = bass.Bass("TRN2", target_bir_lowering=False)
    xT_d = nc.dram_tensor("xT", [128, 8, NTOK], F32, kind="ExternalInput").ap()
    pos_d = nc.dram_tensor("pos", [1, NTOK], I32, kind="ExternalInput").ap()
    wsrc_d = nc.dram_tensor("wsrc", [NBLK * 256, 2048], F32, kind="ExternalInput").ap()
    gvec_d = nc.dram_tensor("gvec", [128, NGC], F32, kind="ExternalInput").ap()
    brow_d = nc.dram_tensor("brow", [1, 512], F32, kind="ExternalInput").ap()
    wsT_d = nc.dram_tensor("wsT", [128, 512], F32, kind="ExternalInput").ap()
    outT_d = nc.dram_tensor("outT", [128, 8, NTOK], F32, kind="ExternalOutput").ap()
    wbf_t = nc.dram_tensor("wbf", [NBLK * 256, 2048], BF16, kind="Internal")
    wbf_d = wbf_t.ap()
    wbf_blk = wbf_d.rearrange("(b p a) n -> b p (a n)", p=128, a=2)
    kc_d = nc.dram_tensor("kcache", [NH, 96, S], BF16, kind="Internal").ap()
    vc_d = nc.dram_tensor("vcache", [NH, 128, NKT, 128], BF16, kind="Internal").ap()

    P = Prog(nc)
    sb = nc.alloc_sbuf_tensor

    xT = [sb("xT%d" % i, [128, 8, T], F32) for i in range(2)]
    xn = sb("xn", [128, 8, T], BF16)
    sq = [sb("sq%d" % i, [128, T], BF16) for i in range(2)]
    rstd_l = [sb("rstd%d" % i, [128, T], F32) for i in range(2)]
    hT = sb("hT", [128, NFC, T], BF16)
    attn = sb("attn", [128, 4, T], F32)
    guT = sb("guT", [128, 4, T], BF16)
    sg = [sb("sg%d" % i, [128, T], BF16) for i in range(2)]
    sgr = [sb("sgr%d" % i, [128, T], BF16) for i in range(2)]
    tg = [sb("tg%d" % i, [128, T], F32) for i in range(2)]
    rstd_big = sb("rstd_big", [128, T], F32)
    rstd_bigF = sb("rstd_bigF", [128, T], F32)
    xnF = sb("xnF", [128, 8, T], BF16)
    wring = sb("wring", [128, NS, BLK], BF16)
    gvec = sb("gvec_sb", [128, NGC], F32)
    brow = sb("brow_sb", [1, 512], F32)
    ones32 = sb("ones32", [128, 128], F32)
    onesb = sb("onesb", [128, 128], BF16)
    ident = sb("ident", [128, 128], BF16)
    wsT = sb("wsT_sb", [128, 4, 128], BF16)
    dummy = sb("fence_dummy", [128, 8], F32)
    zdum = sb("zdum", [128, 512], BF16)
    brow_rep = sb("brow_rep", [1, 4, 4, 128], F32)
    vnT = sb("vnT", [128, 4, T], BF16)
    yAn = vnT
    vtok = sb("vtok", [128, 4, 4, 128], BF16)
    yA = sb("yA", [128, 4, T], F32)
    cqn = sb("cqn", [128, 2, T], BF16)
    ckvn = sb("ckvn", [128, T], BF16)
    kr = sb("kr", [128, T], F32)
    krsw = sb("krsw", [128, T], F32)
    sqr = sb("sqr", [128, T], BF16)
    posi = kr[:, :].bitcast(I32)
    ki = krsw[:, :].bitcast(I32)
    f0 = sb("f0", [128, T], F32)
    f1 = sb("f1", [128, T], F32)
    f2 = sb("f2", [128, T], F32)
    Ct = sb("Ct", [128, T], F32)
    St = sb("St", [128, T], F32)
    invf = sb("invf", [128, 1], F32)
    sgn = sb("sgn", [128, 1], F32)
    iot = sb("iot", [128, 1], I32)
    iof = sb("iof", [128, 1], F32)
    Qt = sb("Qt", [128, NH, T], BF16)
    yBn = Qt
    Kst = xn
    Vst = sb("Vst", [128, NH, 4, 128], BF16)
    PT = [sb("PT%d" % i, [128, T], BF16) for i in range(3)]
    print("sbuf bytes remaining/partition:", nc.sbuf_bytes_remaining)

    psA = [nc.alloc_psum_tensor("psA%d" % i, [128, T], F32) for i in range(4)]
    psB = [nc.alloc_psum_tensor("psB%d" % i, [128, T], F32) for i in range(2)]
    psC = [nc.alloc_psum_tensor("psC%d" % i, [128, T], F32) for i in range(2)]
    cnt = {"A": 0, "B": 0, "C": 0, "T": 0, "sq": 0, "sg": 0, "PT": 0, "ring": 0, "rstd": 0, "sgr": 0, "tg": 0}

    pinned = set()
    pinned_slots = set()
    ffn_pool = ["A"]
    fine_mode = [False]

    def psum(pool):
        lst = {"A": psA, "B": psB, "C": psC}[pool]
        for _ in range(len(lst)):
            i = cnt[pool] % len(lst)
            cnt[pool] += 1
            if ("ps", pool, i) not in pinned:
                return lst[i], ("ps", pool, i)
        raise RuntimeError("no free PSUM bank in pool " + pool)

    def rot(name, lst):
        i = cnt[name] % len(lst)
        cnt[name] += 1
        return lst[i], (name, i)

    def fence(name):
        P.op("pool", lambda e: e.memset(dummy[0:1, 0:1], 0.0), w=[name])

    warm = [None]

    def mm(out, lhsT, rhs, start, stop, r, w):
        P.op("pe", lambda e: e.matmul(out, lhsT=lhsT, rhs=rhs, start=start, stop=stop), r=r, w=w)
        if stop and warm[0] is not None and DUMW:
            jt, jk = warm[0]
            P.op("pe", lambda e: e.matmul(jt[:, 0:DUMW], lhsT=onesb[:, :], rhs=zdum[:, 0:DUMW], start=True, stop=True),
                 r=["onesb", "zdum"], w=[jk])

    def act(out, in_, func, r, w, scale=None, bias=None):
        kw = {}
        if scale is not None:
            kw["scale"] = scale
        if bias is not None:
            kw["bias"] = bias
        P.op("act", lambda e: e.activation(out=out, in_=in_, func=func, **kw), r=r, w=w)

    def stt(eng, out, in0, scalar, in1, op0, op1, r, w):
        P.op(eng, lambda e: e.scalar_tensor_tensor(out=out, in0=in0, scalar=scalar, in1=in1, op0=op0, op1=op1), r=r, w=w)

    def tt(eng, out, in0, in1, op, r, w):
        P.op(eng, lambda e: e.tensor_tensor(out=out, in0=in0, in1=in1, op=op), r=r, w=w)

    def ts(eng, out, in0, s1, s2, op0, op1, r, w):
        if op1 is None:
            P.op(eng, lambda e: e.tensor_scalar(out=out, in0=in0, scalar1=s1, scalar2=None, op0=op0), r=r, w=w)
        else:
            P.op(eng, lambda e: e.tensor_scalar(out=out, in0=in0, scalar1=s1, scalar2=s2, op0=op0, op1=op1), r=r, w=w)

    def finish_rstd(ss_ps, ss_key, np_, scale, bias, pr=slice(0, 128)):
        i = cnt["rstd"] % 2
        cnt["rstd"] += 1
        rstd = rstd_l[i]
        rk_ = ("rstd", i)
        act(rstd[pr, :], ss_ps[pr, :], AF.Ln, r=[ss_key], w=[rk_], scale=scale, bias=bias)
        act(rstd[pr, :], rstd[pr, :], AF.Exp, r=[rk_], w=[rk_], scale=-0.5)
        return rstd, rk_

    def load_blk(b, nelem=BLK, src=None, part=128):
        for _ in range(NS + 1):
            slot = cnt["ring"] % NS
            cnt["ring"] += 1
            if slot not in pinned_slots:
                break
        else:
            raise RuntimeError("no free ring slot")
        key = ("ring", slot)
        if src is None:
            src_ap = wbf_blk[b, :, 0:nelem]
            rk = [("wbf", b)]
        else:
            src_ap, rk = src
        dst = wring[0:part, slot, 0:nelem]
        P.dma("sp", lambda e: e.dma_start(out=dst, in_=src_ap), r=rk, w=[key], key=key)
        return slot, key

    def load_x(ti):
        xt = xT[ti % 2]
        xkeys = [("xT%d" % (ti % 2), c) for c in range(8)]
        tok0 = ti * T
        P.dma("pool", lambda e: e.dma_start(out=xt[:], in_=xT_d[:, :, tok0:tok0 + T]), w=xkeys, key=("xin", ti % 2))

    load_x(0)
    P.dma("sp", lambda e: e.dma_start(out=gvec[:], in_=gvec_d[:, :]), w=["gvec"], key="c_gvec")
    P.dma("sp", lambda e: e.dma_start(out=brow[:], in_=brow_d[:, :]), w=["brow"], key="c_brow")
    P.dma("pool", lambda e: e.dma_start(out=wsT[:].rearrange("p g i -> p (g i)"), in_=wsT_d[:, :]), w=["wsT"], key="c_wsT")
    P.op("pool", lambda e: e.memset(ones32[:], 1.0), w=["ones32"])
    P.op("pool", lambda e: e.memset(zdum[:], 0.0), w=["zdum"])
    for b in range(4):
        P.op("dve", lambda e, b=b: e.tensor_copy(out=brow_rep[0:1, :, b, :], in_=brow[0:1, :].rearrange("p (g i) -> p g i", g=4)),
             r=["brow"], w=["brow_rep"])
    P.op("pool", lambda e: e.memset(onesb[:], 1.0), w=["onesb"])
    P.op("pool", lambda e: e.affine_select(out=ident[:], in_=onesb[:], pattern=[[-1, 128]], compare_op=ALU.is_equal,
                                           fill=0.0, base=0, channel_multiplier=1), r=["onesb"], w=["ident"])
    for b in range(NBLK):
        P.dma("pool", lambda e, b=b: e.dma_start(out=wbf_d[b * 256:(b + 1) * 256, :], in_=wsrc_d[b * 256:(b + 1) * 256, :]),
              w=[("wbf", b)], key=("cv", b))
    P.op("pool", lambda e: e.memset(wsT[64:128, :, 0:64], 0.0), r=["wsT"], w=["wsT"])
    P.op("pool", lambda e: e.memset(Vst[:, :, :, 64:128], 1.0), w=["Vst"])
    P.op("pool", lambda e: e.iota(iot[64:96, :], pattern=[[0, 1]], base=0, channel_multiplier=1), w=["iot"])
    P.op("dve", lambda e: e.tensor_copy(out=iof[64:96, :], in_=iot[64:96, :]), r=["iot"], w=["iof"])
    ts("dve", sgn[64:96, :], iof[64:96, :], 16.0, 2.0, ALU.is_ge, ALU.mult, r=["iof"], w=["sgn"])
    ts("dve", sgn[64:96, :], sgn[64:96, :], -1.0, None, ALU.add, None, r=["sgn"], w=["sgn"])
    ts("dve", invf[64:96, :], iof[64:96, :], 16.0, -16.0, ALU.is_ge, ALU.mult, r=["iof"], w=["invf"])
    tt("dve", iof[64:96, :], iof[64:96, :], invf[64:96, :], ALU.add, r=["iof", "invf"], w=["iof"])
    act(invf[64:96, :], iof[64:96, :], AF.Exp, r=["iof"], w=["invf"], scale=-math.log(10000.0) * 2.0 / 32.0)

    def xg_scale(xt, xkey, gcol, dst=None, dkey="xn"):
        dst = xn if dst is None else dst
        for c in range(8):
            ts("dve", dst[:, c, :], xt[:, c, :], gvec[:, gcol + c:gcol + c + 1], None, ALU.mult, None,
               r=[(xkey, c), "gvec"], w=[(dkey, c)])

    def big_stats(xt, xkey, dst=None, dkey="rstd_big"):
        dst = rstd_big if dst is None else dst
        ss, ssk = psum("C")
        for c in range(8):
            s, sk = rot("sq", sq)
            act(s[:], xt[:, c, :], AF.Square, r=[(xkey, c)], w=[sk])
            mm(ss[:], onesb[:, :], s[:], c == 0, c == 7, r=["onesb", sk], w=[ssk])
        act(dst[:], ss[:], AF.Ln, r=[ssk], w=[dkey], scale=1.0 / D, bias=EPS)
        act(dst[:], dst[:], AF.Exp, r=[dkey], w=[dkey], scale=-0.5)
        return dst, dkey

    def ffn_gen(xt, xkey, gcol, blk0, xsrc, xsk, rdst, rdk, inter=False):
        xg_scale(xt, xkey, gcol, dst=xsrc, dkey=xsk)
        rb = rbk = None
        prev = None

        def finish(pv):
            U, Uk, s2, s2k, f = pv
            tt("dve", hT[:, f, :], U[:], s2[:], ALU.mult, r=[Uk, s2k], w=[("hT", f)])

        for fb in range(11):
            slot, rkey = load_blk(blk0 + fb)
            if inter:
                pinned_slots.add(slot)
            wv = wring[:, slot, :].rearrange("p (f g c m) -> p f g c m", f=2, g=2, c=8)
            for fl in range(2):
                f = fb * 2 + fl
                G, Gk = psum(ffn_pool[0] if inter else "A")
                if inter:
                    pinned.add(Gk)
                for c in range(8):
                    mm(G[:], wv[:, fl, 0, c, :], xsrc[:, c, :], c == 0, c == 7, r=[rkey, (xsk, c)], w=[Gk])
                    if inter and fine_mode[0] and c < 7:
                        yield
                pinned.discard(Gk)
                if f == 0:
                    rb, rbk = big_stats(xt, xkey, dst=rdst, dkey=rdk)
                fine = inter and fine_mode[0]
                t, tk = rot("tg", tg)
                s_, sk = rot("sg", sg)
                if fine:
                    stt("dve", t[:], G[:], 0.5, rb[:], ALU.mult, ALU.mult, r=[Gk, rbk], w=[tk])
                    th, thk = rot("tg", tg)
                    act(th[:], t[:], AF.Tanh, r=[tk], w=[thk])
                    stt("dve", s_[:], th[:], 1.0, t[:], ALU.add, ALU.mult, r=[thk, tk], w=[sk])
                else:
                    tt("dve", t[:], G[:], rb[:], ALU.mult, r=[Gk, rbk], w=[tk])
                    act(s_[:], t[:], AF.Silu, r=[tk], w=[sk])
                s2, s2k = rot("sgr", sgr)
                tt("pool", s2[:], s_[:], rb[:], ALU.mult, r=[sk, rbk], w=[s2k])
                if inter and fine_mode[0]:
                    yield
                U, Uk = psum(ffn_pool[0] if inter else "A")
                if inter:
                    pinned.add(Uk)
                for c in range(8):
                    mm(U[:], wv[:, fl, 1, c, :], xsrc[:, c, :], c == 0, c == 7, r=[rkey, (xsk, c)], w=[Uk])
                    if inter and fine_mode[0] and c < 7:
                        yield
                pinned.discard(Uk)
                if prev is not None:
                    finish(prev)
                prev = (U, Uk, s2, s2k, f)
                if inter and fine_mode[0]:
                    finish(prev)
                    prev = None
                    if fl == 1:
                        pinned_slots.discard(slot)
                    yield
            pinned_slots.discard(slot)
            if not (inter and fine_mode[0]):
                if inter:
                    finish(prev)
                    prev = None
                    yield
                else:
                    yield
        if prev is not None:
            finish(prev)
            prev = None
        for dc in range(8):
            slot, rkey = load_blk(blk0 + 11 + dc, nelem=NFC * 128)
            wv = wring[:, slot, 0:NFC * 128].rearrange("p (f m) -> p f m", f=NFC)
            Y, Yk = psum("B")
            for f in range(NFC):
                mm(Y[:], wv[:, f, :], hT[:, f, :], f == 0, f == NFC - 1, r=[rkey, ("hT", f)], w=[Yk])
            stt("dve", xt[:, dc, :], Y[:], 0.5, xt[:, dc, :], ALU.mult, ALU.add, r=[Yk, (xkey, dc)], w=[(xkey, dc)])
            yield

    R = slice(64, 96)

    def rope_tables(tok0):
        P.dma("pool", lambda e: e.dma_start(out=posi[R, :], in_=pos_d[0:1, tok0:tok0 + T].partition_broadcast(32)),
              w=["kr"], key="posld")
        P.op("dve", lambda e: e.tensor_copy(out=f0[R, :], in_=posi[R, :]), r=["kr"], w=["f0"])
        ts("dve", f0[R, :], f0[R, :], invf[R, :], 1.0 / (2.0 * math.pi), ALU.mult, ALU.mult, r=["f0", "invf"], w=["f0"])
        for (dst, shift, dk) in ((St, 0.0, "St"), (Ct, 0.25, "Ct")):
            if shift != 0.0:
                ts("dve", f1[R, :], f0[R, :], shift, None, ALU.add, None, r=["f0"], w=["f1"])
                src = f1
            else:
                src = f0
            P.op("dve", lambda e, src=src: e.tensor_copy(out=ki[R, :], in_=src[R, :]), r=["f0", "f1"], w=["krsw"])
            P.op("dve", lambda e: e.tensor_copy(out=f2[R, :], in_=ki[R, :]), r=["krsw"], w=["f2"])
            tt("dve", f2[R, :], src[R, :], f2[R, :], ALU.subtract, r=["f0", "f1", "f2"], w=["f2"])
            act(dst[R, :], f2[R, :], AF.Sin, r=["f2"], w=[dk], scale=6.28318)
        ts("dve", St[R, :], St[R, :], sgn[R, :], None, ALU.mult, None, r=["St", "sgn"], w=["St"])

    def mixer(xt, xkey, seq_tile, tok0, blk0, adv, early):
        warm[0] = (psB[0], ("ps", "B", 0))
        fine_mode[0] = True
        ffn_pool[0] = "A"
        xg_scale(xt, xkey, GC_MIX)
        rope_tables(tok0)
        slotA, kA = load_blk(blk0)
        slotB, kB = load_blk(blk0 + 1)
        wA = wring[:, slotA, :].rearrange("p (c n) -> p c n", c=8)
        wB = wring[:, slotB, :].rearrange("p (c n) -> p c n", c=8)

        def proj(wv, kw, c0, c1, pr=slice(0, 128)):
            ps, pk = psum("A")
            for c in range(8):
                mm(ps[pr, :], wv[:, c, c0:c1], xn[:, c, :], c == 0, c == 7, r=[kw, ("xn", c)], w=[pk])
            return ps, pk

        vps = [proj(wB, kB, g * 128, (g + 1) * 128) for g in range(4)]
        rb, rbk = big_stats(xt, xkey)
        for g in range(4):
            ps, pk = vps[g]
            t, tk = rot("tg", tg)
            tt("dve", t[:], ps[:], rb[:], ALU.mult, r=[pk, rbk], w=[tk])
            act(yA[:, g, :], t[:], AF.Gelu_apprx_tanh, r=[tk], w=[("yA", g)])
        for g in range(4):
            ups, upk = proj(wA, kA, g * 128, (g + 1) * 128)
            t, tk = rot("tg", tg)
            tt("dve", t[:], ups[:], rb[:], ALU.mult, r=[upk, rbk], w=[tk])
            act(guT[:, g, :], t[:], AF.Gelu_apprx_tanh, r=[tk], w=[("guT", g)])
        for g in range(4):
            gvg = yA[:, g, :]
            s_, sk = rot("sq", sq)
            act(s_[:], gvg, AF.Square, r=[("yA", g)], w=[sk])
            ss, ssk = psum("C")
            mm(ss[:], onesb[:, :], s_[:], True, True, r=["onesb", sk], w=[ssk])
            rstd, rstdk = finish_rstd(ss, ssk, 128, 1.0 / 128, EPS)
            stt("dve", vnT[:, g, :], gvg, gvec[:, GC_VN + g:GC_VN + g + 1], rstd[:], ALU.mult, ALU.mult,
                r=[("yA", g), "gvec", rstdk], w=[("vnT", g)])

        early()
        slotC, kC = load_blk(blk0 + 2, nelem=8 * 448)
        wC = wring[:, slotC, 0:8 * 448].rearrange("p (c n) -> p c n", c=8)
        cqt = [f0, f1]
        cqk = ["f0", "f1"]
        for j in range(2):
            ps, pk = proj(wC, kC, j * 128, (j + 1) * 128)
            tt("dve", cqt[j][:], ps[:], rb[:], ALU.mult, r=[pk, rbk], w=[cqk[j]])
        ps, pk = proj(wC, kC, 256, 384)
        tt("dve", f2[:], ps[:], rb[:], ALU.mult, r=[pk, rbk], w=["f2"])
        ps, pk = proj(wC, kC, 384, 416, pr=R)
        tt("dve", kr[R, :], ps[R, :], rb[R, :], ALU.mult, r=[pk, rbk], w=["kr"])
        ps, pk = proj(wC, kC, 416, 448, pr=R)
        tt("dve", krsw[R, :], ps[R, :], rb[R, :], ALU.mult, r=[pk, rbk], w=["krsw"])
        for g in range(4):
            tps, tk = psum("C")
            tpb = tps[:, :].bitcast(BF16)
            for b in range(4):
                P.op("pe", lambda e, g=g, b=b, tpb=tpb: e.transpose(tpb[:, b * 128:(b + 1) * 128],
                                                                    vnT[:, g, b * 128:(b + 1) * 128], ident[:]),
                     r=[("vnT", g), "ident"], w=[tk])
            P.op("dve", lambda e, g=g, tpb=tpb: e.tensor_copy(out=vtok[:, :, g, :],
                                                              in_=tpb[:, 0:T].rearrange("p (b c) -> p b c", b=4)),
                 r=[tk], w=[("vtok", g)])
        ss, ssk = psum("C")
        for j in range(2):
            s_, sk = rot("sq", sq)
            act(s_[:], cqt[j][:], AF.Square, r=[cqk[j]], w=[sk])
            mm(ss[:], onesb[:, :], s_[:], j == 0, j == 1, r=["onesb", sk], w=[ssk])
        rstd, rstdk = finish_rstd(ss, ssk, 128, 1.0 / 256, EPS)
        for j in range(2):
            stt("dve", cqn[:, j, :], cqt[j][:], gvec[:, GC_QL + j:GC_QL + j + 1], rstd[:], ALU.mult, ALU.mult,
                r=[cqk[j], "gvec", rstdk], w=[("cqn", j)])
        s_, sk = rot("sq", sq)
        act(s_[:], f2[:], AF.Square, r=["f2"], w=[sk])
        ss, ssk = psum("C")
        mm(ss[:], onesb[:, :], s_[:], True, True, r=["onesb", sk], w=[ssk])
        rstd, rstdk = finish_rstd(ss, ssk, 128, 1.0 / 128, EPS)
        stt("dve", ckvn[:], f2[:], gvec[:, GC_KVL:GC_KVL + 1], rstd[:], ALU.mult, ALU.mult,
            r=["f2", "gvec", rstdk], w=["ckvn"])
        act(sqr[R, :], kr[R, :], AF.Square, r=["kr"], w=["sqr"])
        stt("dve", kr[R, :], kr[R, :], gvec[R, GC_KH:GC_KH + 1], Ct[R, :], ALU.mult, ALU.mult, r=["kr", "gvec", "Ct"], w=["kr"])
        stt("dve", krsw[R, :], krsw[R, :], gvec[R, GC_KHS:GC_KHS + 1], St[R, :], ALU.mult, ALU.mult, r=["krsw", "gvec", "St"], w=["krsw"])
        tt("pool", kr[R, :], kr[R, :], krsw[R, :], ALU.add, r=["kr", "krsw"], w=["kr"])
        for g in range(4):
            ps, pk = psum("A")
            mm(ps[:, :], ones32[0:1, :], brow_rep[0:1, g, :, :].rearrange("p b i -> p (b i)"), True, False,
               r=["ones32", "brow_rep"], w=[pk])
            for b in range(4):
                mm(ps[:, b * 128:(b + 1) * 128], vtok[:, b, g, :], wsT[:, g, :], False, b == 3,
                   r=[("vtok", g), "wsT"], w=[pk])
            tt("dve", yA[:, g, :], ps[:], guT[:, g, :], ALU.mult, r=[pk, ("guT", g)], w=[("yA", g)])
        ss, ssk = psum("C")
        for g in range(4):
            s, sk = rot("sq", sq)
            act(s[:], yA[:, g, :], AF.Square, r=[("yA", g)], w=[sk])
            mm(ss[:], onesb[:, :], s[:], g == 0, g == 3, r=["onesb", sk], w=[ssk])
        rstd, rstdk = finish_rstd(ss, ssk, 128, 1.0 / 512, EPS)
        for g in range(4):
            eng = "dve"
            stt(eng, yAn[:, g, :], yA[:, g, :], gvec[:, GC_GO + g:GC_GO + g + 1], rstd[:], ALU.mult, ALU.mult,
                r=[("yA", g), "gvec", rstdk], w=[("vnT", g)])

        slotM, kM = load_blk(None, nelem=1024, src=(wbf_blk[blk0 + 3, :, 2048:3072], [("wbf", blk0 + 3)]))
        wKV = wring[:, slotM, 0:1024]
        for b in range(4):
            ps, pk = psum("A")
            mm(ps[:], ckvn[:, b * 128:(b + 1) * 128], wKV[:, 512:1024], True, True, r=["ckvn", kM], w=[pk])
            act(Vst[:, :, b, 0:64], ps[:].rearrange("p (h e) -> p h e", h=NH), AF.Copy, r=[pk], w=["Vst"])
        kst = {}
        for i_ in range(2):
            P.op("pool", lambda e, i_=i_: e.tensor_copy(out=sq[i_][R, :], in_=sqr[R, :]), r=["sqr", ("sq", i_)], w=[("sq", i_)])

        def k_A(h):
            ps, pk = psum("A")
            mm(ps[0:64, :], wKV[:, h * 64:(h + 1) * 64], ckvn[:], True, True, r=[kM, "ckvn"], w=[pk])
            s_, sk = rot("sq", sq)
            act(s_[0:64, :], ps[0:64, :], AF.Square, r=[pk], w=[sk])
            kst[h] = [ps, pk, s_, sk]

        def k_B(h):
            ps, pk, s_, sk = kst[h]
            ss, ssk = psum("C")
            mm(ss[0:96, :], onesb[0:96, 0:96], s_[0:96, :], True, True, r=["onesb", sk], w=[ssk])
            rstd, rstdk = finish_rstd(ss, ssk, 96, 1.0 / 96, EPS, pr=slice(0, 96))
            kst[h] += [rstd, rstdk]

        def k_C(h):
            ps, pk, s_, sk, rstd, rstdk = kst[h]
            stt("dve", Kst[0:64, h, :], ps[0:64, :], gvec[0:64, GC_KH:GC_KH + 1], rstd[0:64, :], ALU.mult, ALU.mult,
                r=[pk, "gvec", rstdk], w=[("xn", h)])
            tt("pool", Kst[R, h, :], kr[R, :], rstd[R, :], ALU.mult, r=["kr", rstdk], w=[("xn", h)])

        for step in range(NH + 2):
            if step < NH:
                k_A(step)
            if 1 <= step <= NH:
                k_B(step - 1)
            if step >= 2:
                k_C(step - 2)
        kt0 = seq_tile * 4
        P.dma("pool", lambda e: e.dma_start(out=kc_d[:, :, seq_tile * T:(seq_tile + 1) * T].rearrange("h p t -> p h t"),
                                            in_=Kst[0:96, :, :]), r=[("xn", c) for c in range(8)], w=["kc"], key="kst")
        P.dma("pool", lambda e: e.dma_start(out=vc_d[:, :, kt0:kt0 + 4, :].rearrange("h p k e -> p h (k e)"),
                                            in_=Vst[:, :, :, :].rearrange("p h k e -> p h (k e)")), r=["Vst"], w=["vc"], key="vst")
        slotQ, kQ = load_blk(None, nelem=2048, src=(wbf_blk[blk0 + 3, :, 0:2048], [("wbf", blk0 + 3)]))
        wUQ = wring[:, slotQ, 0:2048].rearrange("p (c n) -> p c n", c=2)
        qst = {}

        def q_A(h):
            ps, pk = psum("A")
            for j in range(2):
                mm(ps[0:96, :], wUQ[:, j, h * 96:(h + 1) * 96], cqn[:, j, :], j == 0, j == 1, r=[kQ, ("cqn", j)], w=[pk])
            ps2, pk2 = psum("A")
            for j in range(2):
                mm(ps2[R, :], wUQ[:, j, 768 + h * 32:768 + (h + 1) * 32], cqn[:, j, :], j == 0, j == 1,
                   r=[kQ, ("cqn", j)], w=[pk2])
            s_, sk = rot("sq", sq)
            act(s_[0:96, :], ps[0:96, :], AF.Square, r=[pk], w=[sk])
            qst[h] = (ps, pk, ps2, pk2, s_, sk)

        def q_B(h):
            ps, pk, ps2, pk2, s_, sk = qst[h]
            ra, rak = (f0, "f0") if h % 2 == 0 else (kr, "kr")
            rb_, rbk_ = (f1, "f1") if h % 2 == 0 else (krsw, "krsw")
            qn, qnk = rot("tg", tg)
            act(qn[0:64, :], ps[0:64, :], AF.Copy, r=[pk, "gvec"], w=[qnk], scale=gvec[0:64, GC_QH:GC_QH + 1])
            stt("dve", ra[R, :], ps[R, :], gvec[R, GC_QH:GC_QH + 1], Ct[R, :], ALU.mult, ALU.mult, r=[pk, "gvec", "Ct"], w=[rak])
            stt("dve", rb_[R, :], ps2[R, :], gvec[R, GC_QHS:GC_QHS + 1], St[R, :], ALU.mult, ALU.mult, r=[pk2, "gvec", "St"], w=[rbk_])
            ss, ssk = psum("C")
            mm(ss[0:96, :], onesb[0:96, 0:96], s_[0:96, :], True, True, r=["onesb", sk], w=[ssk])
            rstd, rstdk = finish_rstd(ss, ssk, 96, 1.0, 96.0 * EPS, pr=slice(0, 96))
            tt("pool", ra[R, :], ra[R, :], rb_[R, :], ALU.add, r=[rak, rbk_], w=[rak])
            tt("dve", Qt[0:64, h, :], qn[0:64, :], rstd[0:64, :], ALU.mult, r=[qnk, rstdk], w=[("Qt", h)])
            tt("pool", Qt[R, h, :], ra[R, :], rstd[R, :], ALU.mult, r=[rak, rstdk], w=[("Qt", h)])

        q_A(0)
        for h in range(NH):
            if h + 1 < NH:
                q_A(h + 1)
            q_B(h)
        warm[0] = None
        adv(16)

        warm[0] = None
        nk = (seq_tile + 1) * T
        nkt = nk // 128
        epi = []

        def flush_epi():
            while epi:
                h_, O, Ok = epi.pop(0)
                pinned.discard(Ok)
                P.op("dve", lambda e, O=O: e.reciprocal(out=f2[0:64, :], in_=O[64:128, :]), r=[Ok], w=["f2"])
                hr = slice(0, 64) if h_ < 4 else slice(64, 128)
                tt("dve", attn[hr, h_ % 4, :], O[0:64, :], f2[0:64, :], ALU.mult, r=[Ok, "f2"], w=[("attn", h_)])

        for h in range(NH):
            ks, kk = load_blk(None, nelem=nk, src=(kc_d[h, :, 0:nk], ["kc"]), part=96)
            vs, vk = load_blk(None, nelem=nkt * 128, src=(vc_d[h, :, 0:nkt, :].rearrange("p k e -> p (k e)"), ["vc"]))
            pinned_slots.add(ks)
            pinned_slots.add(vs)
            Kv = wring[0:96, ks, 0:nk]
            Vv = wring[:, vs, 0:nkt * 128].rearrange("p (k e) -> p k e", e=128)
            O, Ok = psum("B")
            pinned.add(Ok)
            pend = []

            def issue_s(kt):
                q0 = max(0, kt - seq_tile * 4) * 128
                Sps, Sk = psum("A")
                mm(Sps[:, q0:T], Kv[:, kt * 128:(kt + 1) * 128], Qt[0:96, h, q0:T], True, True, r=[kk, ("Qt", h)], w=[Sk])
                pend.append((kt, q0, Sps, Sk))
                pinned.add(Sk)

            LOOK = 3
            for kt in range(min(LOOK, nkt)):
                issue_s(kt)
            flush_epi()
            for kt in range(nkt):
                _, q0, Sps, Sk = pend.pop(0)
                p, pkk = rot("PT", PT)
                act(p[:, q0:T], Sps[:, q0:T], AF.Exp, r=[Sk], w=[pkk])
                pinned.discard(Sk)
                if kt >= seq_tile * 4:
                    P.op("pool", lambda e, p=p, q0=q0: e.memset(p[64:128, q0:q0 + 64], 0.0), r=[pkk], w=[pkk])
                mm(O[:, q0:T], Vv[:, kt, :], p[:, q0:T], kt == 0, kt == nkt - 1, r=[vk, pkk], w=[Ok])
                if kt + LOOK < nkt:
                    issue_s(kt + LOOK)
                if DUMN:
                    P.op("pe", lambda e: e.matmul(psC[0][:, 0:DUMN], lhsT=onesb[:, :], rhs=zdum[:, 0:DUMN], start=True, stop=True),
                         r=["onesb", "zdum"], w=[("ps", "C", 0)])
            epi.append((h, O, Ok))
            pinned_slots.discard(ks)
            pinned_slots.discard(vs)
        flush_epi()
        ffn_pool[0] = "A"
        fine_mode[0] = False
        adv(2)
        ss, ssk = psum("C")
        for h in range(NH):
            hr = slice(0, 64) if h < 4 else slice(64, 128)
            s_, sk = rot("sq", sq)
            act(s_[hr, :], attn[hr, h % 4, :], AF.Square, r=[("attn", h)], w=[sk])
            mm(ss[:, :], onesb[hr, :], s_[hr, :], h == 0, h == NH - 1, r=["onesb", sk], w=[ssk])
        rstd, rstdk = finish_rstd(ss, ssk, 128, 1.0 / 512, EPS)
        for h in range(NH):
            hr = slice(0, 64) if h < 4 else slice(64, 128)
            stt("dve", yBn[0:64, h, :], attn[hr, h % 4, :], gvec[hr, GC_MO + h % 4:GC_MO + h % 4 + 1], rstd[hr, :],
                ALU.mult, ALU.mult, r=[("attn", h), "gvec", rstdk], w=[("Qt", h)])
        adv(100000)
        warm[0] = (psA[0], ("ps", "A", 0))
        for ob in range(4):
            slot, rkey = load_blk(blk0 + 4 + ob, nelem=3072)
            wv = wring[:, slot, 0:3072].rearrange("p (d k m) -> p d k m", d=2, k=12)
            for dl in range(2):
                dc = ob * 2 + dl
                Y, Yk = psum("B")
                for g in range(4):
                    mm(Y[:], wv[:, dl, g, :], yAn[:, g, :], g == 0, False, r=[rkey, ("vnT", g)], w=[Yk])
                for h in range(NH):
                    mm(Y[:], wv[0:64, dl, 4 + h, :], yBn[0:64, h, :], False, h == NH - 1, r=[rkey, ("Qt", h)], w=[Yk])
                tt("dve", xt[:, dc, :], Y[:], xt[:, dc, :], ALU.add, r=[Yk, (xkey, dc)], w=[(xkey, dc)])
        warm[0] = None

    def final_norm(xt, xkey):
        ss, ssk = psum("C")
        for c in range(8):
            s, sk = rot("sq", sq)
            act(s[:], xt[:, c, :], AF.Square, r=[(xkey, c)], w=[sk])
            mm(ss[:], onesb[:, :], s[:], c == 0, c == 7, r=["onesb", sk], w=[ssk])
        rstd, rstdk = finish_rstd(ss, ssk, 128, 1.0 / D, EPS)
        for c in range(8):
            eng = "dve"
            stt(eng, xt[:, c, :], xt[:, c, :], gvec[:, GC_FIN + c:GC_FIN + c + 1], rstd[:], ALU.mult, ALU.mult,
                r=[(xkey, c), "gvec", rstdk], w=[(xkey, c)])

    finals = []
    ntiles = NSEQ * NTS

    def run_all(gen):
        for _ in gen:
            pass

    def xinfo(ti):
        return xT[ti % 2], "xT%d" % (ti % 2), [("xT%d" % (ti % 2), c) for c in range(8)]

    def fin(ti):
        xt, xkey, xkeys = xinfo(ti)
        tok0 = ti * T
        final_norm(xt, xkey)
        ref = P.dma("pool", lambda e: e.dma_start(out=outT_d[:, :, tok0:tok0 + T], in_=xt[:]),
                    r=xkeys, key=("xout", ti % 2))
        finals.append(ref)

    xt0, xk0, _ = xinfo(0)
    run_all(ffn_gen(xt0, xk0, GC_FFN1, 0, xnF, "xnF", rstd_bigF, "rstd_bigF"))
    for ti in range(ntiles):
        seq, st = divmod(ti, NTS)
        xt, xkey, xkeys = xinfo(ti)
        def early(ti=ti):
            if ti >= 1:
                fin(ti - 1)
            if ti + 1 < ntiles:
                load_x(ti + 1)

        if ti + 1 < ntiles:
            xt1, xk1, _ = xinfo(ti + 1)
            filler = ffn_gen(xt1, xk1, GC_FFN1, 0, xnF, "xnF", rstd_bigF, "rstd_bigF", inter=True)
        else:
            filler = iter(())

        def adv(n, filler=filler, site=None):
            import os
            mask = os.environ.get("ADV_SITES")
            if mask is not None and n < 99 and str(site) not in mask.split(","):
                return
            for _ in range(n):
                try:
                    next(filler)
                except StopIteration:
                    return

        mixer(xt, xkey, st, ti * T, 19, adv, early)
        run_all(ffn_gen(xt, xkey, GC_FFN2, 27, xn, "xn", rstd_big, "rstd_big"))
    fin(ntiles - 1)
    P.emit(final_waits=finals)
    return nc


def _partner():
    j = np.arange(32)
    return np.where(j < 16, j + 16, j - 16)


def _pad(a):
    out = np.zeros((a.shape[0], 128, BLK), np.float32)
    out[:, :a.shape[1], :a.shape[2]] = a
    return out


def _gu_blocks(Wg, Wu):
    g = Wg.reshape(8, 128, 11, 2, 128)
    u = Wu.reshape(8, 128, 11, 2, 128)
    a = np.stack([g, u], 0)
    a = a.transpose(3, 2, 4, 0, 1, 5)
    return a.reshape(11, 128, 4096)


def _wd_blocks(Wd):
    a = Wd.reshape(NFC, 128, 8, 128).transpose(2, 1, 0, 3)
    return _pad(a.reshape(8, 128, NFC * 128))


def _layout_weights(inp):
    pt = _partner()
    blocks = []
    blocks.append(_gu_blocks(inp["ffn1_w_gate"][0], inp["ffn1_w_up"][0]))
    blocks.append(_wd_blocks(inp["ffn1_w_down"][0]))
    Win = inp["w_in"][0]
    blocks.append(Win[:, 0:512].reshape(8, 128, 512).transpose(1, 0, 2).reshape(1, 128, 4096))
    blocks.append(Win[:, 512:1024].reshape(8, 128, 512).transpose(1, 0, 2).reshape(1, 128, 4096))
    wc = np.concatenate([Win[:, 1024:1440], Win[:, 1408 + pt]], axis=1)
    blocks.append(_pad(wc.reshape(8, 128, 448).transpose(1, 0, 2).reshape(1, 128, 8 * 448)))
    wuq = inp["w_uq"][0]
    sw = np.concatenate([wuq[:, h * 96 + 64 + pt] for h in range(NH)], axis=1)
    uq = np.concatenate([wuq, sw], axis=1).reshape(2, 128, 1024).transpose(1, 0, 2).reshape(128, 2048)
    wkv = inp["w_ukv"][0].reshape(128, NH, 128)
    kv = np.concatenate([wkv[:, :, 0:64].reshape(128, 512), wkv[:, :, 64:128].reshape(128, 512)], axis=1)
    blocks.append(_pad(np.concatenate([uq, kv], axis=1)[None]))
    Wo = inp["w_out"][0]
    wo = np.zeros((4, 128, 2, 12, 128), np.float32)
    for ob in range(4):
        for dl in range(2):
            dc = ob * 2 + dl
            for g in range(4):
                wo[ob, :, dl, g, :] = Wo[g * 128:(g + 1) * 128, dc * 128:(dc + 1) * 128]
            for h in range(NH):
                wo[ob, 0:64, dl, 4 + h, :] = Wo[512 + h * 64:512 + (h + 1) * 64, dc * 128:(dc + 1) * 128]
    blocks.append(_pad(wo.reshape(4, 128, 3072)))
    blocks.append(_gu_blocks(inp["ffn2_w_gate"][0], inp["ffn2_w_up"][0]))
    blocks.append(_wd_blocks(inp["ffn2_w_down"][0]))
    w = np.concatenate(blocks, axis=0)
    assert w.shape == (NBLK, 128, BLK), w.shape
    return np.ascontiguousarray(w.reshape(NBLK * 256, 2048))


def _layout_gvec(inp):
    pt = _partner()
    g = np.zeros((128, NGC), np.float32)
    for col, name in ((GC_FFN1, "ffn1_norm"), (GC_MIX, "mix_norm"), (GC_FFN2, "ffn2_norm"), (GC_FIN, "final_norm")):
        g[:, col:col + 8] = inp[name][0].reshape(8, 128).T
    g[:, GC_VN:GC_VN + 4] = inp["gmlp_v_norm"][0].T
    g[:, GC_QL:GC_QL + 2] = inp["q_latent_norm"][0].reshape(2, 128).T
    g[:, GC_KVL] = inp["kv_latent_norm"][0]
    qh = inp["q_head_norm"][0]
    kh = inp["k_head_norm"][0]
    g[0:96, GC_QH] = qh
    g[0:96, GC_KH] = kh
    g[64:96, GC_QHS] = qh[64 + pt]
    g[64:96, GC_KHS] = kh[64 + pt]
    g[:, GC_GO:GC_GO + 4] = inp["gmlp_out_norm"][0].reshape(4, 128).T
    mo = inp["mla_out_norm"][0].reshape(8, 64)
    g[0:64, GC_MO:GC_MO + 4] = mo[0:4].T
    g[64:128, GC_MO:GC_MO + 4] = mo[4:8].T
    return g


def _prep_shared(inp):
    inp = {k: np.asarray(v) for k, v in inp.items()}
    wsrc = _layout_weights(inp)
    gvec = _layout_gvec(inp)
    brow = np.ascontiguousarray(inp["gmlp_b_s"][0].reshape(1, 512).astype(np.float32))
    wsT = np.ascontiguousarray(inp["gmlp_w_s"][0].transpose(2, 0, 1).reshape(128, 512))
    return dict(wsrc=wsrc, gvec=gvec, brow=brow, wsT=wsT)


def _run(inputs, n_cores, NSEQ, S):
    x = np.asarray(inputs["x"])
    pos = np.asarray(inputs["positions"]).astype(np.int32)
    shared = _prep_shared(inputs)
    nc = build_nc(NSEQ, S)
    in_maps = []
    for c in range(n_cores):
        xs = x[c * NSEQ:(c + 1) * NSEQ].reshape(NSEQ * S, 8, 128)
        xTh = np.ascontiguousarray(xs.transpose(2, 1, 0))
        ps = np.ascontiguousarray(pos[c * NSEQ:(c + 1) * NSEQ].reshape(1, NSEQ * S))
        m = dict(shared)
        m["xT"] = xTh
        m["pos"] = ps
        in_maps.append(m)
    res = run_bass_kernel_spmd(nc, in_maps, core_ids=list(range(n_cores)))
    outs = []
    for c in range(n_cores):
        o = res.results[c]["outT"]
        outs.append(np.ascontiguousarray(o.transpose(2, 1, 0)).reshape(NSEQ, S, D))
    return np.concatenate(outs, axis=0).astype(np.float32)


def kernel(**inputs):
    return _run(inputs, 8, 2, 4096)
```

```python
import math
import numpy as np
import concourse.bass as bass
import concourse.mybir as mybir
from concourse.bass_utils import run_bass_kernel_spmd

F32 = mybir.dt.float32
BF16 = mybir.dt.bfloat16
I32 = mybir.dt.int32
AF = mybir.ActivationFunctionType
ALU = mybir.AluOpType

ENGS = ("pe", "act", "dve", "pool", "sp")

D = 1024
FF = 2816
NFC = 22
T = 512
EPS = 1e-6
NBLK = 46
BLK = 4096
NS = 5
NH = 8
DUMN = 256
DUMW = 512


class Prog:
    def __init__(self, nc):
        self.nc = nc
        self.ops = {e: [] for e in ENGS}
        self.last_w = {}
        self.readers = {}
        self.dma_cnt = {}

    def op(self, eng, fn, r=(), w=()):
        return self._add(eng, fn, r, w, None)

    def dma(self, eng, fn, r=(), w=(), key=None):
        self.dma_cnt[key] = self.dma_cnt.get(key, 0) + 16
        return self._add(eng, fn, r, w, (key, self.dma_cnt[key]))

    def _add(self, eng, fn, r, w, dma):
        deps = set()
        for k in r:
            if k in self.last_w:
                deps.add(self.last_w[k])
        for k in w:
            if k in self.last_w:
                deps.add(self.last_w[k])
            for x in self.readers.get(k, ()):
                deps.add(x)
        ref = (eng, len(self.ops[eng]))
        deps.discard(ref)
        self.ops[eng].append(dict(fn=fn, deps=deps, dma=dma))
        for k in r:
            self.readers.setdefault(k, []).append(ref)
        for k in w:
            self.last_w[k] = ref
            self.readers[k] = []
        return ref

    def emit(self, final_waits=()):
        nc = self.nc
        needed = {e: set() for e in ENGS}
        for e in ENGS:
            for o in self.ops[e]:
                for (de, di) in o["deps"]:
                    if self.ops[de][di]["dma"] is None:
                        if de == "pe" and e == "pe":
                            continue
                        needed[de].add(di)
        rank = {}
        for e in ENGS:
            c = 0
            for i in range(len(self.ops[e])):
                if i in needed[e]:
                    c += 1
                    rank[(e, i)] = c
        sems = {e: nc.alloc_semaphore("s_" + e) for e in ENGS}
        dsems = {k: nc.alloc_semaphore("d_%d" % i) for i, k in enumerate(self.dma_cnt)}

        def target(ref):
            o = self.ops[ref[0]][ref[1]]
            if o["dma"] is not None:
                return dsems[o["dma"][0]], o["dma"][1]
            return sems[ref[0]], rank[ref]

        def run(eng_name, handle):
            seen = {}
            for i, o in enumerate(self.ops[eng_name]):
                for d in sorted(o["deps"]):
                    if d[0] == "pe" and eng_name == "pe" and self.ops[d[0]][d[1]]["dma"] is None:
                        continue
                    s, v = target(d)
                    if seen.get(id(s), 0) < v:
                        handle.wait_ge(s, v)
                        seen[id(s)] = v
                inst = o["fn"](handle)
                if o["dma"] is not None:
                    inst.then_inc(dsems[o["dma"][0]], 16)
                elif (eng_name, i) in rank:
                    inst.then_inc(sems[eng_name], 1)
            if eng_name == "sp":
                for ref in final_waits:
                    s, v = target(ref)
                    handle.wait_ge(s, v)

        with nc.Block() as block:
            @block.tensor
            def _(h):
                run("pe", h)

            @block.scalar
            def _(h):
                run("act", h)

            @block.vector
            def _(h):
                run("dve", h)

            @block.gpsimd
            def _(h):
                run("pool", h)

            @block.sync
            def _(h):
                run("sp", h)


GC_FFN1, GC_MIX, GC_FFN2, GC_FIN = 0, 8, 16, 24
GC_VN, GC_QL, GC_KVL = 32, 36, 38
GC_QH, GC_QHS, GC_KH, GC_KHS = 39, 40, 41, 42
GC_GO, GC_MO = 43, 47
NGC = 55


def build_nc(NSEQ, S):
    NTOK = NSEQ * S
    NTS = S // T
    NKT = S // 128
    nc = bass.Bass("TRN2", target_bir_lowering=False)
    xT_d = nc.dram_tensor("xT", [128, 8, NTOK], F32, kind="ExternalInput").ap()
    pos_d = nc.dram_tensor("pos", [1, NTOK], I32, kind="ExternalInput").ap()
    wsrc_d = nc.dram_tensor("wsrc", [NBLK * 256, 2048], F32, kind="ExternalInput").ap()
    gvec_d = nc.dram_tensor("gvec", [128, NGC], F32, kind="ExternalInput").ap()
    brow_d = nc.dram_tensor("brow", [1, 512], F32, kind="ExternalInput").ap()
    wsT_d = nc.dram_tensor("wsT", [128, 512], F32, kind="ExternalInput").ap()
    outT_d = nc.dram_tensor("outT", [128, 8, NTOK], F32, kind="ExternalOutput").ap()
    wbf_t = nc.dram_tensor("wbf", [NBLK * 256, 2048], BF16, kind="Internal")
    wbf_d = wbf_t.ap()
    wbf_blk = wbf_d.rearrange("(b p a) n -> b p (a n)", p=128, a=2)
    kc_d = nc.dram_tensor("kcache", [NH, 96, S], BF16, kind="Internal").ap()
    vc_d = nc.dram_tensor("vcache", [NH, 128, NKT, 128], BF16, kind="Internal").ap()

    P = Prog(nc)
    sb = nc.alloc_sbuf_tensor

    xT = [sb("xT%d" % i, [128, 8, T], F32) for i in range(2)]
    xn = sb("xn", [128, 8, T], BF16)
    sq = [sb("sq%d" % i, [128, T], BF16) for i in range(2)]
    rstd_l = [sb("rstd%d" % i, [128, T], F32) for i in range(2)]
    hT = sb("hT", [128, NFC, T], BF16)
    attn = sb("attn", [128, 4, T], F32)
    guT = sb("guT", [128, 4, T], BF16)
    sg = [sb("sg%d" % i, [128, T], BF16) for i in range(2)]
    sgr = [sb("sgr%d" % i, [128, T], BF16) for i in range(2)]
    tg = [sb("tg%d" % i, [128, T], F32) for i in range(2)]
    rstd_big = sb("rstd_big", [128, T], F32)
    rstd_bigF = sb("rstd_bigF", [128, T], F32)
    xnF = sb("xnF", [128, 8, T], BF16)
    wring = sb("wring", [128, NS, BLK], BF16)
    gvec = sb("gvec_sb", [128, NGC], F32)
    brow = sb("brow_sb", [1, 512], F32)
    ones32 = sb("ones32", [128, 128], F32)
    onesb = sb("onesb", [128, 128], BF16)
    ident = sb("ident", [128, 128], BF16)
    wsT = sb("wsT_sb", [128, 4, 128], BF16)
    dummy = sb("fence_dummy", [128, 8], F32)
    zdum = sb("zdum", [128, 512], BF16)
    brow_rep = sb("brow_rep", [1, 4, 4, 128], F32)
    vnT = sb("vnT", [128, 4, T], BF16)
    yAn = vnT
    vtok = sb("vtok", [128, 4, 4, 128], BF16)
    yA = sb("yA", [128, 4, T], F32)
    cqn = sb("cqn", [128, 2, T], BF16)
    ckvn = sb("ckvn", [128, T], BF16)
    kr = sb("kr", [128, T], F32)
    krsw = sb("krsw", [128, T], F32)
    sqr = sb("sqr", [128, T], BF16)
    posi = kr[:, :].bitcast(I32)
    ki = krsw[:, :].bitcast(I32)
    f0 = sb("f0", [128, T], F32)
    f1 = sb("f1", [128, T], F32)
    f2 = sb("f2", [128, T], F32)
    Ct = sb("Ct", [128, T], F32)
    St = sb("St", [128, T], F32)
    invf = sb("invf", [128, 1], F32)
    sgn = sb("sgn", [128, 1], F32)
    iot = sb("iot", [128, 1], I32)
    iof = sb("iof", [128, 1], F32)
    Qt = sb("Qt", [128, NH, T], BF16)
    yBn = Qt
    Kst = xn
    Vst = sb("Vst", [128, NH, 4, 128], BF16)
    PT = [sb("PT%d" % i, [128, T], BF16) for i in range(3)]
    print("sbuf bytes remaining/partition:", nc.sbuf_bytes_remaining)

    psA = [nc.alloc_psum_tensor("psA%d" % i, [128, T], F32) for i in range(4)]
    psB = [nc.alloc_psum_tensor("psB%d" % i, [128, T], F32) for i in range(2)]
    psC = [nc.alloc_psum_tensor("psC%d" % i, [128, T], F32) for i in range(2)]
    cnt = {"A": 0, "B": 0, "C": 0, "T": 0, "sq": 0, "sg": 0, "PT": 0, "ring": 0, "rstd": 0, "sgr": 0, "tg": 0}

    pinned = set()
    pinned_slots = set()
    ffn_pool = ["A"]
    fine_mode = [False]

    def psum(pool):
        lst = {"A": psA, "B": psB, "C": psC}[pool]
        for _ in range(len(lst)):
            i = cnt[pool] % len(lst)
            cnt[pool] += 1
            if ("ps", pool, i) not in pinned:
                return lst[i], ("ps", pool, i)
        raise RuntimeError("no free PSUM bank in pool " + pool)

    def rot(name, lst):
        i = cnt[name] % len(lst)
        cnt[name] += 1
        return lst[i], (name, i)

    def fence(name):
        P.op("pool", lambda e: e.memset(dummy[0:1, 0:1], 0.0), w=[name])

    warm = [None]

    def mm(out, lhsT, rhs, start, stop, r, w):
        P.op("pe", lambda e: e.matmul(out, lhsT=lhsT, rhs=rhs, start=start, stop=stop), r=r, w=w)
        if stop and warm[0] is not None and DUMW:
            jt, jk = warm[0]
            P.op("pe", lambda e: e.matmul(jt[:, 0:DUMW], lhsT=onesb[:, :], rhs=zdum[:, 0:DUMW], start=True, stop=True),
                 r=["onesb", "zdum"], w=[jk])

    def act(out, in_, func, r, w, scale=None, bias=None):
        kw = {}
        if scale is not None:
            kw["scale"] = scale
        if bias is not None:
            kw["bias"] = bias
        P.op("act", lambda e: e.activation(out=out, in_=in_, func=func, **kw), r=r, w=w)

    def stt(eng, out, in0, scalar, in1, op0, op1, r, w):
        P.op(eng, lambda e: e.scalar_tensor_tensor(out=out, in0=in0, scalar=scalar, in1=in1, op0=op0, op1=op1), r=r, w=w)

    def tt(eng, out, in0, in1, op, r, w):
        P.op(eng, lambda e: e.tensor_tensor(out=out, in0=in0, in1=in1, op=op), r=r, w=w)

    def ts(eng, out, in0, s1, s2, op0, op1, r, w):
        if op1 is None:
            P.op(eng, lambda e: e.tensor_scalar(out=out, in0=in0, scalar1=s1, scalar2=None, op0=op0), r=r, w=w)
        else:
            P.op(eng, lambda e: e.tensor_scalar(out=out, in0=in0, scalar1=s1, scalar2=s2, op0=op0, op1=op1), r=r, w=w)

    def finish_rstd(ss_ps, ss_key, np_, scale, bias, pr=slice(0, 128)):
        i = cnt["rstd"] % 2
        cnt["rstd"] += 1
        rstd = rstd_l[i]
        rk_ = ("rstd", i)
        act(rstd[pr, :], ss_ps[pr, :], AF.Ln, r=[ss_key], w=[rk_], scale=scale, bias=bias)
        act(rstd[pr, :], rstd[pr, :], AF.Exp, r=[rk_], w=[rk_], scale=-0.5)
        return rstd, rk_

    def load_blk(b, nelem=BLK, src=None, part=128):
        for _ in range(NS + 1):
            slot = cnt["ring"] % NS
            cnt["ring"] += 1
            if slot not in pinned_slots:
                break
        else:
            raise RuntimeError("no free ring slot")
        key = ("ring", slot)
        if src is None:
            src_ap = wbf_blk[b, :, 0:nelem]
            rk = [("wbf", b)]
        else:
            src_ap, rk = src
        dst = wring[0:part, slot, 0:nelem]
        P.dma("sp", lambda e: e.dma_start(out=dst, in_=src_ap), r=rk, w=[key], key=key)
        return slot, key

    def load_x(ti):
        xt = xT[ti % 2]
        xkeys = [("xT%d" % (ti % 2), c) for c in range(8)]
        tok0 = ti * T
        P.dma("pool", lambda e: e.dma_start(out=xt[:], in_=xT_d[:, :, tok0:tok0 + T]), w=xkeys, key=("xin", ti % 2))

    load_x(0)
    P.dma("sp", lambda e: e.dma_start(out=gvec[:], in_=gvec_d[:, :]), w=["gvec"], key="c_gvec")
    P.dma("sp", lambda e: e.dma_start(out=brow[:], in_=brow_d[:, :]), w=["brow"], key="c_brow")
    P.dma("pool", lambda e: e.dma_start(out=wsT[:].rearrange("p g i -> p (g i)"), in_=wsT_d[:, :]), w=["wsT"], key="c_wsT")
    P.op("pool", lambda e: e.memset(ones32[:], 1.0), w=["ones32"])
    P.op("pool", lambda e: e.memset(zdum[:], 0.0), w=["zdum"])
    for b in range(4):
        P.op("dve", lambda e, b=b: e.tensor_copy(out=brow_rep[0:1, :, b, :], in_=brow[0:1, :].rearrange("p (g i) -> p g i", g=4)),
             r=["brow"], w=["brow_rep"])
    P.op("pool", lambda e: e.memset(onesb[:], 1.0), w=["onesb"])
    P.op("pool", lambda e: e.affine_select(out=ident[:], in_=onesb[:], pattern=[[-1, 128]], compare_op=ALU.is_equal,
                                           fill=0.0, base=0, channel_multiplier=1), r=["onesb"], w=["ident"])
    for b in range(NBLK):
        P.dma("pool", lambda e, b=b: e.dma_start(out=wbf_d[b * 256:(b + 1) * 256, :], in_=wsrc_d[b * 256:(b + 1) * 256, :]),
              w=[("wbf", b)], key=("cv", b))
    P.op("pool", lambda e: e.memset(wsT[64:128, :, 0:64], 0.0), r=["wsT"], w=["wsT"])
    P.op("pool", lambda e: e.memset(Vst[:, :, :, 64:128], 1.0), w=["Vst"])
    P.op("pool", lambda e: e.iota(iot[64:96, :], pattern=[[0, 1]], base=0, channel_multiplier=1), w=["iot"])
    P.op("dve", lambda e: e.tensor_copy(out=iof[64:96, :], in_=iot[64:96, :]), r=["iot"], w=["iof"])
    ts("dve", sgn[64:96, :], iof[64:96, :], 16.0, 2.0, ALU.is_ge, ALU.mult, r=["iof"], w=["sgn"])
    ts("dve", sgn[64:96, :], sgn[64:96, :], -1.0, None, ALU.add, None, r=["sgn"], w=["sgn"])
    ts("dve", invf[64:96, :], iof[64:96, :], 16.0, -16.0, ALU.is_ge, ALU.mult, r=["iof"], w=["invf"])
    tt("dve", iof[64:96, :], iof[64:96, :], invf[64:96, :], ALU.add, r=["iof", "invf"], w=["iof"])
    act(invf[64:96, :], iof[64:96, :], AF.Exp, r=["iof"], w=["invf"], scale=-math.log(10000.0) * 2.0 / 32.0)

    def xg_scale(xt, xkey, gcol, dst=None, dkey="xn"):
        dst = xn if dst is None else dst
        for c in range(8):
            ts("dve", dst[:, c, :], xt[:, c, :], gvec[:, gcol + c:gcol + c + 1], None, ALU.mult, None,
               r=[(xkey, c), "gvec"], w=[(dkey, c)])

    def big_stats(xt, xkey, dst=None, dkey="rstd_big"):
        dst = rstd_big if dst is None else dst
        ss, ssk = psum("C")
        for c in range(8):
            s, sk = rot("sq", sq)
            act(s[:], xt[:, c, :], AF.Square, r=[(xkey, c)], w=[sk])
            mm(ss[:], onesb[:, :], s[:], c == 0, c == 7, r=["onesb", sk], w=[ssk])
        act(dst[:], ss[:], AF.Ln, r=[ssk], w=[dkey], scale=1.0 / D, bias=EPS)
        act(dst[:], dst[:], AF.Exp, r=[dkey], w=[dkey], scale=-0.5)
        return dst, dkey

    def ffn_gen(xt, xkey, gcol, blk0, xsrc, xsk, rdst, rdk, inter=False):
        xg_scale(xt, xkey, gcol, dst=xsrc, dkey=xsk)
        rb = rbk = None
        prev = None

        def finish(pv):
            U, Uk, s2, s2k, f = pv
            tt("dve", hT[:, f, :], U[:], s2[:], ALU.mult, r=[Uk, s2k], w=[("hT", f)])

        for fb in range(11):
            slot, rkey = load_blk(blk0 + fb)
            if inter:
                pinned_slots.add(slot)
            wv = wring[:, slot, :].rearrange("p (f g c m) -> p f g c m", f=2, g=2, c=8)
            for fl in range(2):
                f = fb * 2 + fl
                G, Gk = psum(ffn_pool[0] if inter else "A")
                if inter:
                    pinned.add(Gk)
                for c in range(8):
                    mm(G[:], wv[:, fl, 0, c, :], xsrc[:, c, :], c == 0, c == 7, r=[rkey, (xsk, c)], w=[Gk])
                    if inter and fine_mode[0] and c < 7:
                        yield
                pinned.discard(Gk)
                if f == 0:
                    rb, rbk = big_stats(xt, xkey, dst=rdst, dkey=rdk)
                fine = inter and fine_mode[0]
                t, tk = rot("tg", tg)
                s_, sk = rot("sg", sg)
                if fine:
                    stt("dve", t[:], G[:], 0.5, rb[:], ALU.mult, ALU.mult, r=[Gk, rbk], w=[tk])
                    th, thk = rot("tg", tg)
                    act(th[:], t[:], AF.Tanh, r=[tk], w=[thk])
                    stt("dve", s_[:], th[:], 1.0, t[:], ALU.add, ALU.mult, r=[thk, tk], w=[sk])
                else:
                    tt("dve", t[:], G[:], rb[:], ALU.mult, r=[Gk, rbk], w=[tk])
                    act(s_[:], t[:], AF.Silu, r=[tk], w=[sk])
                s2, s2k = rot("sgr", sgr)
                tt("pool", s2[:], s_[:], rb[:], ALU.mult, r=[sk, rbk], w=[s2k])
                if inter and fine_mode[0]:
                    yield
                U, Uk = psum(ffn_pool[0] if inter else "A")
                if inter:
                    pinned.add(Uk)
                for c in range(8):
                    mm(U[:], wv[:, fl, 1, c, :], xsrc[:, c, :], c == 0, c == 7, r=[rkey, (xsk, c)], w=[Uk])
                    if inter and fine_mode[0] and c < 7:
                        yield
                pinned.discard(Uk)
                if prev is not None:
                    finish(prev)
                prev = (U, Uk, s2, s2k, f)
                if inter and fine_mode[0]:
                    finish(prev)
                    prev = None
                    if fl == 1:
                        pinned_slots.discard(slot)
                    yield
            pinned_slots.discard(slot)
            if not (inter and fine_mode[0]):
                if inter:
                    finish(prev)
                    prev = None
                    yield
                else:
                    yield
        if prev is not None:
            finish(prev)
            prev = None
        for dc in range(8):
            slot, rkey = load_blk(blk0 + 11 + dc, nelem=NFC * 128)
            wv = wring[:, slot, 0:NFC * 128].rearrange("p (f m) -> p f m", f=NFC)
            Y, Yk = psum("B")
            for f in range(NFC):
                mm(Y[:], wv[:, f, :], hT[:, f, :], f == 0, f == NFC - 1, r=[rkey, ("hT", f)], w=[Yk])
            stt("dve", xt[:, dc, :], Y[:], 0.5, xt[:, dc, :], ALU.mult, ALU.add, r=[Yk, (xkey, dc)], w=[(xkey, dc)])
            yield

    R = slice(64, 96)

    def rope_tables(tok0):
        P.dma("pool", lambda e: e.dma_start(out=posi[R, :], in_=pos_d[0:1, tok0:tok0 + T].partition_broadcast(32)),
              w=["kr"], key="posld")
        P.op("dve", lambda e: e.tensor_copy(out=f0[R, :], in_=posi[R, :]), r=["kr"], w=["f0"])
        ts("dve", f0[R, :], f0[R, :], invf[R, :], 1.0 / (2.0 * math.pi), ALU.mult, ALU.mult, r=["f0", "invf"], w=["f0"])
        for (dst, shift, dk) in ((St, 0.0, "St"), (Ct, 0.25, "Ct")):
            if shift != 0.0:
                ts("dve", f1[R, :], f0[R, :], shift, None, ALU.add, None, r=["f0"], w=["f1"])
                src = f1
            else:
                src = f0
            P.op("dve", lambda e, src=src: e.tensor_copy(out=ki[R, :], in_=src[R, :]), r=["f0", "f1"], w=["krsw"])
            P.op("dve", lambda e: e.tensor_copy(out=f2[R, :], in_=ki[R, :]), r=["krsw"], w=["f2"])
            tt("dve", f2[R, :], src[R, :], f2[R, :], ALU.subtract, r=["f0", "f1", "f2"], w=["f2"])
            act(dst[R, :], f2[R, :], AF.Sin, r=["f2"], w=[dk], scale=6.28318)
        ts("dve", St[R, :], St[R, :], sgn[R, :], None, ALU.mult, None, r=["St", "sgn"], w=["St"])

    def mixer(xt, xkey, seq_tile, tok0, blk0, adv, early):
        warm[0] = (psB[0], ("ps", "B", 0))
        fine_mode[0] = True
        ffn_pool[0] = "A"
        xg_scale(xt, xkey, GC_MIX)
        rope_tables(tok0)
        slotA, kA = load_blk(blk0)
        slotB, kB = load_blk(blk0 + 1)
        wA = wring[:, slotA, :].rearrange("p (c n) -> p c n", c=8)
        wB = wring[:, slotB, :].rearrange("p (c n) -> p c n", c=8)

        def proj(wv, kw, c0, c1, pr=slice(0, 128)):
            ps, pk = psum("A")
            for c in range(8):
                mm(ps[pr, :], wv[:, c, c0:c1], xn[:, c, :], c == 0, c == 7, r=[kw, ("xn", c)], w=[pk])
            return ps, pk

        vps = [proj(wB, kB, g * 128, (g + 1) * 128) for g in range(4)]
        rb, rbk = big_stats(xt, xkey)
        for g in range(4):
            ps, pk = vps[g]
            t, tk = rot("tg", tg)
            tt("dve", t[:], ps[:], rb[:], ALU.mult, r=[pk, rbk], w=[tk])
            act(yA[:, g, :], t[:], AF.Gelu_apprx_tanh, r=[tk], w=[("yA", g)])
        for g in range(4):
            ups, upk = proj(wA, kA, g * 128, (g + 1) * 128)
            t, tk = rot("tg", tg)
            tt("dve", t[:], ups[:], rb[:], ALU.mult, r=[upk, rbk], w=[tk])
            act(guT[:, g, :], t[:], AF.Gelu_apprx_tanh, r=[tk], w=[("guT", g)])
        for g in range(4):
            gvg = yA[:, g, :]
            s_, sk = rot("sq", sq)
            act(s_[:], gvg, AF.Square, r=[("yA", g)], w=[sk])
            ss, ssk = psum("C")
            mm(ss[:], onesb[:, :], s_[:], True, True, r=["onesb", sk], w=[ssk])
            rstd, rstdk = finish_rstd(ss, ssk, 128, 1.0 / 128, EPS)
            stt("dve", vnT[:, g, :], gvg, gvec[:, GC_VN + g:GC_VN + g + 1], rstd[:], ALU.mult, ALU.mult,
                r=[("yA", g), "gvec", rstdk], w=[("vnT", g)])

        early()
        slotC, kC = load_blk(blk0 + 2, nelem=8 * 448)
        wC = wring[:, slotC, 0:8 * 448].rearrange("p (c n) -> p c n", c=8)
        cqt = [f0, f1]
        cqk = ["f0", "f1"]
        for j in range(2):
            ps, pk = proj(wC, kC, j * 128, (j + 1) * 128)
            tt("dve", cqt[j][:], ps[:], rb[:], ALU.mult, r=[pk, rbk], w=[cqk[j]])
        ps, pk = proj(wC, kC, 256, 384)
        tt("dve", f2[:], ps[:], rb[:], ALU.mult, r=[pk, rbk], w=["f2"])
        ps, pk = proj(wC, kC, 384, 416, pr=R)
        tt("dve", kr[R, :], ps[R, :], rb[R, :], ALU.mult, r=[pk, rbk], w=["kr"])
        ps, pk = proj(wC, kC, 416, 448, pr=R)
        tt("dve", krsw[R, :], ps[R, :], rb[R, :], ALU.mult, r=[pk, rbk], w=["krsw"])
        for g in range(4):
            tps, tk = psum("C")
            tpb = tps[:, :].bitcast(BF16)
            for b in range(4):
                P.op("pe", lambda e, g=g, b=b, tpb=tpb: e.transpose(tpb[:, b * 128:(b + 1) * 128],
                                                                    vnT[:, g, b * 128:(b + 1) * 128], ident[:]),
                     r=[("vnT", g), "ident"], w=[tk])
            P.op("dve", lambda e, g=g, tpb=tpb: e.tensor_copy(out=vtok[:, :, g, :],
                                                              in_=tpb[:, 0:T].rearrange("p (b c) -> p b c", b=4)),
                 r=[tk], w=[("vtok", g)])
        ss, ssk = psum("C")
        for j in range(2):
            s_, sk = rot("sq", sq)
            act(s_[:], cqt[j][:], AF.Square, r=[cqk[j]], w=[sk])
            mm(ss[:], onesb[:, :], s_[:], j == 0, j == 1, r=["onesb", sk], w=[ssk])
        rstd, rstdk = finish_rstd(ss, ssk, 128, 1.0 / 256, EPS)
        for j in range(2):
            stt("dve", cqn[:, j, :], cqt[j][:], gvec[:, GC_QL + j:GC_QL + j + 1], rstd[:], ALU.mult, ALU.mult,
                r=[cqk[j], "gvec", rstdk], w=[("cqn", j)])
        s_, sk = rot("sq", sq)
        act(s_[:], f2[:], AF.Square, r=["f2"], w=[sk])
        ss, ssk = psum("C")
        mm(ss[:], onesb[:, :], s_[:], True, True, r=["onesb", sk], w=[ssk])
        rstd, rstdk = finish_rstd(ss, ssk, 128, 1.0 / 128, EPS)
        stt("dve", ckvn[:], f2[:], gvec[:, GC_KVL:GC_KVL + 1], rstd[:], ALU.mult, ALU.mult,
            r=["f2", "gvec", rstdk], w=["ckvn"])
        act(sqr[R, :], kr[R, :], AF.Square, r=["kr"], w=["sqr"])
        stt("dve", kr[R, :], kr[R, :], gvec[R, GC_KH:GC_KH + 1], Ct[R, :], ALU.mult, ALU.mult, r=["kr", "gvec", "Ct"], w=["kr"])
        stt("dve", krsw[R, :], krsw[R, :], gvec[R, GC_KHS:GC_KHS + 1], St[R, :], ALU.mult, ALU.mult, r=["krsw", "gvec", "St"], w=["krsw"])
        tt("pool", kr[R, :], kr[R, :], krsw[R, :], ALU.add, r=["kr", "krsw"], w=["kr"])
        for g in range(4):
            ps, pk = psum("A")
            mm(ps[:, :], ones32[0:1, :], brow_rep[0:1, g, :, :].rearrange("p b i -> p (b i)"), True, False,
               r=["ones32", "brow_rep"], w=[pk])
            for b in range(4):
                mm(ps[:, b * 128:(b + 1) * 128], vtok[:, b, g, :], wsT[:, g, :], False, b == 3,
                   r=[("vtok", g), "wsT"], w=[pk])
            tt("dve", yA[:, g, :], ps[:], guT[:, g, :], ALU.mult, r=[pk, ("guT", g)], w=[("yA", g)])
        ss, ssk = psum("C")
        for g in range(4):
            s, sk = rot("sq", sq)
            act(s[:], yA[:, g, :], AF.Square, r=[("yA", g)], w=[sk])
            mm(ss[:], onesb[:, :], s[:], g == 0, g == 3, r=["onesb", sk], w=[ssk])
        rstd, rstdk = finish_rstd(ss, ssk, 128, 1.0 / 512, EPS)
        for g in range(4):
            eng = "dve"
            stt(eng, yAn[:, g, :], yA[:, g, :], gvec[:, GC_GO + g:GC_GO + g + 1], rstd[:], ALU.mult, ALU.mult,
                r=[("yA", g), "gvec", rstdk], w=[("vnT", g)])

        slotM, kM = load_blk(None, nelem=1024, src=(wbf_blk[blk0 + 3, :, 2048:3072], [("wbf", blk0 + 3)]))
        wKV = wring[:, slotM, 0:1024]
        for b in range(4):
            ps, pk = psum("A")
            mm(ps[:], ckvn[:, b * 128:(b + 1) * 128], wKV[:, 512:1024], True, True, r=["ckvn", kM], w=[pk])
            act(Vst[:, :, b, 0:64], ps[:].rearrange("p (h e) -> p h e", h=NH), AF.Copy, r=[pk], w=["Vst"])
        warm[0] = None
        slotQ, kQ = load_blk(None, nelem=2048, src=(wbf_blk[blk0 + 3, :, 0:2048], [("wbf", blk0 + 3)]))
        wUQ = wring[:, slotQ, 0:2048].rearrange("p (c n) -> p c n", c=2)
        kst = {}
        for i_ in range(2):
            P.op("pool", lambda e, i_=i_: e.tensor_copy(out=sgr[i_][R, :], in_=sqr[R, :]), r=["sqr", ("sgr", i_)], w=[("sgr", i_)])

        def k_A(h):
            ps, pk = psum("B")
            mm(ps[0:64, :], wKV[:, h * 64:(h + 1) * 64], ckvn[:], True, True, r=[kM, "ckvn"], w=[pk])
            s_, sk = rot("sgr", sgr)
            act(s_[0:64, :], ps[0:64, :], AF.Square, r=[pk], w=[sk])
            kst[h] = [ps, pk, s_, sk]

        def k_B(h):
            ps, pk, s_, sk = kst[h]
            ss, ssk = psum("C")
            mm(ss[0:96, :], onesb[0:96, 0:96], s_[0:96, :], True, True, r=["onesb", sk], w=[ssk])
            rstd, rstdk = finish_rstd(ss, ssk, 96, 1.0 / 96, EPS, pr=slice(0, 96))
            kst[h] += [rstd, rstdk]

        def k_C(h):
            ps, pk, s_, sk, rstd, rstdk = kst[h]
            stt("dve", Kst[0:64, h, :], ps[0:64, :], gvec[0:64, GC_KH:GC_KH + 1], rstd[0:64, :], ALU.mult, ALU.mult,
                r=[pk, "gvec", rstdk], w=[("xn", h)])
            tt("pool", Kst[R, h, :], kr[R, :], rstd[R, :], ALU.mult, r=["kr", rstdk], w=[("xn", h)])

        qst = {}

        def q_A(h):
            ps, pk = psum("A")
            for j in range(2):
                mm(ps[0:96, :], wUQ[:, j, h * 96:(h + 1) * 96], cqn[:, j, :], j == 0, j == 1, r=[kQ, ("cqn", j)], w=[pk])
            ps2, pk2 = psum("A")
            for j in range(2):
                mm(ps2[R, :], wUQ[:, j, 768 + h * 32:768 + (h + 1) * 32], cqn[:, j, :], j == 0, j == 1,
                   r=[kQ, ("cqn", j)], w=[pk2])
            s_, sk = rot("sq", sq)
            act(s_[0:96, :], ps[0:96, :], AF.Square, r=[pk], w=[sk])
            qst[h] = (ps, pk, ps2, pk2, s_, sk)

        def q_B(h):
            ps, pk, ps2, pk2, s_, sk = qst[h]
            ra, rak = (f0, "f0") if h % 2 == 0 else (yA[:, 0, :], ("yA", 0))
            rb_, rbk_ = (f1, "f1") if h % 2 == 0 else (yA[:, 1, :], ("yA", 1))
            qn, qnk = rot("tg", tg)
            act(qn[0:64, :], ps[0:64, :], AF.Copy, r=[pk, "gvec"], w=[qnk], scale=gvec[0:64, GC_QH:GC_QH + 1])
            stt("dve", ra[R, :], ps[R, :], gvec[R, GC_QH:GC_QH + 1], Ct[R, :], ALU.mult, ALU.mult, r=[pk, "gvec", "Ct"], w=[rak])
            stt("dve", rb_[R, :], ps2[R, :], gvec[R, GC_QHS:GC_QHS + 1], St[R, :], ALU.mult, ALU.mult, r=[pk2, "gvec", "St"], w=[rbk_])
            ss, ssk = psum("C")
            mm(ss[0:96, :], onesb[0:96, 0:96], s_[0:96, :], True, True, r=["onesb", sk], w=[ssk])
            rstd, rstdk = finish_rstd(ss, ssk, 96, 1.0, 96.0 * EPS, pr=slice(0, 96))
            tt("pool", ra[R, :], ra[R, :], rb_[R, :], ALU.add, r=[rak, rbk_], w=[rak])
            tt("dve", Qt[0:64, h, :], qn[0:64, :], rstd[0:64, :], ALU.mult, r=[qnk, rstdk], w=[("Qt", h)])
            tt("pool", Qt[R, h, :], ra[R, :], rstd[R, :], ALU.mult, r=[rak, rstdk], w=[("Qt", h)])

        kt0 = seq_tile * 4
        k_A(0)
        k_A(1)
        q_A(0)
        for h in range(NH):
            kk = [k for k in (2 * h, 2 * h + 1) if k < NH]
            for k in kk:
                k_B(k)
            if h + 1 < NH:
                q_A(h + 1)
            for k in kk:
                k_C(k)
            for k in kk:
                if k + 2 < NH:
                    k_A(k + 2)
            if kk and kk[-1] == NH - 1:
                P.dma("pool", lambda e: e.dma_start(out=kc_d[:, :, seq_tile * T:(seq_tile + 1) * T].rearrange("h p t -> p h t"),
                                                    in_=Kst[0:96, :, :]), r=[("xn", c) for c in range(8)], w=["kc"], key="kst")
                P.dma("pool", lambda e: e.dma_start(out=vc_d[:, :, kt0:kt0 + 4, :].rearrange("h p k e -> p h (k e)"),
                                                    in_=Vst[:, :, :, :].rearrange("p h k e -> p h (k e)")), r=["Vst"], w=["vc"], key="vst")
            q_B(h)
        warm[0] = None
        adv(16)

        warm[0] = None
        nk = (seq_tile + 1) * T
        nkt = nk // 128
        epi = []

        def flush_epi():
            while epi:
                h_, O, Ok = epi.pop(0)
                pinned.discard(Ok)
                P.op("dve", lambda e, O=O: e.reciprocal(out=f2[0:64, :], in_=O[64:128, :]), r=[Ok], w=["f2"])
                hr = slice(0, 64) if h_ < 4 else slice(64, 128)
                tt("dve", attn[hr, h_ % 4, :], O[0:64, :], f2[0:64, :], ALU.mult, r=[Ok, "f2"], w=[("attn", h_)])

        for h in range(NH):
            ks, kk = load_blk(None, nelem=nk, src=(kc_d[h, :, 0:nk], ["kc"]), part=96)
            vs, vk = load_blk(None, nelem=nkt * 128, src=(vc_d[h, :, 0:nkt, :].rearrange("p k e -> p (k e)"), ["vc"]))
            pinned_slots.add(ks)
            pinned_slots.add(vs)
            Kv = wring[0:96, ks, 0:nk]
            Vv = wring[:, vs, 0:nkt * 128].rearrange("p (k e) -> p k e", e=128)
            O, Ok = psum("B")
            pinned.add(Ok)
            pend = []

            def issue_s(kt):
                q0 = max(0, kt - seq_tile * 4) * 128
                Sps, Sk = psum("A")
                mm(Sps[:, q0:T], Kv[:, kt * 128:(kt + 1) * 128], Qt[0:96, h, q0:T], True, True, r=[kk, ("Qt", h)], w=[Sk])
                pend.append((kt, q0, Sps, Sk))
                pinned.add(Sk)

            LOOK = 3
            for kt in range(min(LOOK, nkt)):
                issue_s(kt)
            flush_epi()
            for kt in range(nkt):
                _, q0, Sps, Sk = pend.pop(0)
                p, pkk = rot("PT", PT)
                act(p[:, q0:T], Sps[:, q0:T], AF.Exp, r=[Sk], w=[pkk])
                pinned.discard(Sk)
                if kt >= seq_tile * 4:
                    P.op("pool", lambda e, p=p, q0=q0: e.memset(p[64:128, q0:q0 + 64], 0.0), r=[pkk], w=[pkk])
                mm(O[:, q0:T], Vv[:, kt, :], p[:, q0:T], kt == 0, kt == nkt - 1, r=[vk, pkk], w=[Ok])
                if kt + LOOK < nkt:
                    issue_s(kt + LOOK)
                if DUMN:
                    P.op("pe", lambda e: e.matmul(psC[0][:, 0:DUMN], lhsT=onesb[:, :], rhs=zdum[:, 0:DUMN], start=True, stop=True),
                         r=["onesb", "zdum"], w=[("ps", "C", 0)])
            epi.append((h, O, Ok))
            pinned_slots.discard(ks)
            pinned_slots.discard(vs)
        flush_epi()
        ffn_pool[0] = "A"
        fine_mode[0] = False
        adv(2)
        ss, ssk = psum("C")
        for h in range(NH):
            hr = slice(0, 64) if h < 4 else slice(64, 128)
            s_, sk = rot("sq", sq)
            act(s_[hr, :], attn[hr, h % 4, :], AF.Square, r=[("attn", h)], w=[sk])
            mm(ss[:, :], onesb[hr, :], s_[hr, :], h == 0, h == NH - 1, r=["onesb", sk], w=[ssk])
        rstd, rstdk = finish_rstd(ss, ssk, 128, 1.0 / 512, EPS)
        for h in range(NH):
            hr = slice(0, 64) if h < 4 else slice(64, 128)
            stt("dve", yBn[0:64, h, :], attn[hr, h % 4, :], gvec[hr, GC_MO + h % 4:GC_MO + h % 4 + 1], rstd[hr, :],
                ALU.mult, ALU.mult, r=[("attn", h), "gvec", rstdk], w=[("Qt", h)])
        adv(100000)
        warm[0] = (psA[0], ("ps", "A", 0))
        for ob in range(4):
            slot, rkey = load_blk(blk0 + 4 + ob, nelem=3072)
            wv = wring[:, slot, 0:3072].rearrange("p (d k m) -> p d k m", d=2, k=12)
            for dl in range(2):
                dc = ob * 2 + dl
                Y, Yk = psum("B")
                for g in range(4):
                    mm(Y[:], wv[:, dl, g, :], yAn[:, g, :], g == 0, False, r=[rkey, ("vnT", g)], w=[Yk])
                for h in range(NH):
                    mm(Y[:], wv[0:64, dl, 4 + h, :], yBn[0:64, h, :], False, h == NH - 1, r=[rkey, ("Qt", h)], w=[Yk])
                tt("dve", xt[:, dc, :], Y[:], xt[:, dc, :], ALU.add, r=[Yk, (xkey, dc)], w=[(xkey, dc)])
        warm[0] = None

    def final_norm(xt, xkey):
        ss, ssk = psum("C")
        for c in range(8):
            s, sk = rot("sq", sq)
            act(s[:], xt[:, c, :], AF.Square, r=[(xkey, c)], w=[sk])
            mm(ss[:], onesb[:, :], s[:], c == 0, c == 7, r=["onesb", sk], w=[ssk])
        rstd, rstdk = finish_rstd(ss, ssk, 128, 1.0 / D, EPS)
        for c in range(8):
            eng = "dve"
            stt(eng, xt[:, c, :], xt[:, c, :], gvec[:, GC_FIN + c:GC_FIN + c + 1], rstd[:], ALU.mult, ALU.mult,
                r=[(xkey, c), "gvec", rstdk], w=[(xkey, c)])

    finals = []
    ntiles = NSEQ * NTS

    def run_all(gen):
        for _ in gen:
            pass

    def xinfo(ti):
        return xT[ti % 2], "xT%d" % (ti % 2), [("xT%d" % (ti % 2), c) for c in range(8)]

    def fin(ti):
        xt, xkey, xkeys = xinfo(ti)
        tok0 = ti * T
        final_norm(xt, xkey)
        ref = P.dma("pool", lambda e: e.dma_start(out=outT_d[:, :, tok0:tok0 + T], in_=xt[:]),
                    r=xkeys, key=("xout", ti % 2))
        finals.append(ref)

    xt0, xk0, _ = xinfo(0)
    run_all(ffn_gen(xt0, xk0, GC_FFN1, 0, xnF, "xnF", rstd_bigF, "rstd_bigF"))
    for ti in range(ntiles):
        seq, st = divmod(ti, NTS)
        xt, xkey, xkeys = xinfo(ti)
        def early(ti=ti):
            if ti >= 1:
                fin(ti - 1)
            if ti + 1 < ntiles:
                load_x(ti + 1)

        if ti + 1 < ntiles:
            xt1, xk1, _ = xinfo(ti + 1)
            filler = ffn_gen(xt1, xk1, GC_FFN1, 0, xnF, "xnF", rstd_bigF, "rstd_bigF", inter=True)
        else:
            filler = iter(())

        def adv(n, filler=filler, site=None):
            import os
            mask = os.environ.get("ADV_SITES")
            if mask is not None and n < 99 and str(site) not in mask.split(","):
                return
            for _ in range(n):
                try:
                    next(filler)
                except StopIteration:
                    return

        mixer(xt, xkey, st, ti * T, 19, adv, early)
        run_all(ffn_gen(xt, xkey, GC_FFN2, 27, xn, "xn", rstd_big, "rstd_big"))
    fin(ntiles - 1)
    P.emit(final_waits=finals)
    return nc


def _partner():
    j = np.arange(32)
    return np.where(j < 16, j + 16, j - 16)


def _pad(a):
    out = np.zeros((a.shape[0], 128, BLK), np.float32)
    out[:, :a.shape[1], :a.shape[2]] = a
    return out


def _gu_blocks(Wg, Wu):
    g = Wg.reshape(8, 128, 11, 2, 128)
    u = Wu.reshape(8, 128, 11, 2, 128)
    a = np.stack([g, u], 0)
    a = a.transpose(3, 2, 4, 0, 1, 5)
    return a.reshape(11, 128, 4096)


def _wd_blocks(Wd):
    a = Wd.reshape(NFC, 128, 8, 128).transpose(2, 1, 0, 3)
    return _pad(a.reshape(8, 128, NFC * 128))


def _layout_weights(inp):
    pt = _partner()
    blocks = []
    blocks.append(_gu_blocks(inp["ffn1_w_gate"][0], inp["ffn1_w_up"][0]))
    blocks.append(_wd_blocks(inp["ffn1_w_down"][0]))
    Win = inp["w_in"][0]
    blocks.append(Win[:, 0:512].reshape(8, 128, 512).transpose(1, 0, 2).reshape(1, 128, 4096))
    blocks.append(Win[:, 512:1024].reshape(8, 128, 512).transpose(1, 0, 2).reshape(1, 128, 4096))
    wc = np.concatenate([Win[:, 1024:1440], Win[:, 1408 + pt]], axis=1)
    blocks.append(_pad(wc.reshape(8, 128, 448).transpose(1, 0, 2).reshape(1, 128, 8 * 448)))
    wuq = inp["w_uq"][0]
    sw = np.concatenate([wuq[:, h * 96 + 64 + pt] for h in range(NH)], axis=1)
    uq = np.concatenate([wuq, sw], axis=1).reshape(2, 128, 1024).transpose(1, 0, 2).reshape(128, 2048)
    wkv = inp["w_ukv"][0].reshape(128, NH, 128)
    kv = np.concatenate([wkv[:, :, 0:64].reshape(128, 512), wkv[:, :, 64:128].reshape(128, 512)], axis=1)
    blocks.append(_pad(np.concatenate([uq, kv], axis=1)[None]))
    Wo = inp["w_out"][0]
    wo = np.zeros((4, 128, 2, 12, 128), np.float32)
    for ob in range(4):
        for dl in range(2):
            dc = ob * 2 + dl
            for g in range(4):
                wo[ob, :, dl, g, :] = Wo[g * 128:(g + 1) * 128, dc * 128:(dc + 1) * 128]
            for h in range(NH):
                wo[ob, 0:64, dl, 4 + h, :] = Wo[512 + h * 64:512 + (h + 1) * 64, dc * 128:(dc + 1) * 128]
    blocks.append(_pad(wo.reshape(4, 128, 3072)))
    blocks.append(_gu_blocks(inp["ffn2_w_gate"][0], inp["ffn2_w_up"][0]))
    blocks.append(_wd_blocks(inp["ffn2_w_down"][0]))
    w = np.concatenate(blocks, axis=0)
    assert w.shape == (NBLK, 128, BLK), w.shape
    return np.ascontiguousarray(w.reshape(NBLK * 256, 2048))


def _layout_gvec(inp):
    pt = _partner()
    g = np.zeros((128, NGC), np.float32)
    for col, name in ((GC_FFN1, "ffn1_norm"), (GC_MIX, "mix_norm"), (GC_FFN2, "ffn2_norm"), (GC_FIN, "final_norm")):
        g[:, col:col + 8] = inp[name][0].reshape(8, 128).T
    g[:, GC_VN:GC_VN + 4] = inp["gmlp_v_norm"][0].T
    g[:, GC_QL:GC_QL + 2] = inp["q_latent_norm"][0].reshape(2, 128).T
    g[:, GC_KVL] = inp["kv_latent_norm"][0]
    qh = inp["q_head_norm"][0]
    kh = inp["k_head_norm"][0]
    g[0:96, GC_QH] = qh
    g[0:96, GC_KH] = kh
    g[64:96, GC_QHS] = qh[64 + pt]
    g[64:96, GC_KHS] = kh[64 + pt]
    g[:, GC_GO:GC_GO + 4] = inp["gmlp_out_norm"][0].reshape(4, 128).T
    mo = inp["mla_out_norm"][0].reshape(8, 64)
    g[0:64, GC_MO:GC_MO + 4] = mo[0:4].T
    g[64:128, GC_MO:GC_MO + 4] = mo[4:8].T
    return g


def _prep_shared(inp):
    inp = {k: np.asarray(v) for k, v in inp.items()}
    wsrc = _layout_weights(inp)
    gvec = _layout_gvec(inp)
    brow = np.ascontiguousarray(inp["gmlp_b_s"][0].reshape(1, 512).astype(np.float32))
    wsT = np.ascontiguousarray(inp["gmlp_w_s"][0].transpose(2, 0, 1).reshape(128, 512))
    return dict(wsrc=wsrc, gvec=gvec, brow=brow, wsT=wsT)


def _run(inputs, n_cores, NSEQ, S):
    x = np.asarray(inputs["x"])
    pos = np.asarray(inputs["positions"]).astype(np.int32)
    shared = _prep_shared(inputs)
    nc = build_nc(NSEQ, S)
    in_maps = []
    for c in range(n_cores):
        xs = x[c * NSEQ:(c + 1) * NSEQ].reshape(NSEQ * S, 8, 128)
        xTh = np.ascontiguousarray(xs.transpose(2, 1, 0))
        ps = np.ascontiguousarray(pos[c * NSEQ:(c + 1) * NSEQ].reshape(1, NSEQ * S))
        m = dict(shared)
        m["xT"] = xTh
        m["pos"] = ps
        in_maps.append(m)
    res = run_bass_kernel_spmd(nc, in_maps, core_ids=list(range(n_cores)))
    outs = []
    for c in range(n_cores):
        o = res.results[c]["outT"]
        outs.append(np.ascontiguousarray(o.transpose(2, 1, 0)).reshape(NSEQ, S, D))
    return np.concatenate(outs, axis=0).astype(np.float32)


def kernel(**inputs):
    return _run(inputs, 8, 2, 4096)
```
